# Optimizing a Trainium2 kernel written in Bass

```python
import math
import jax, jax.numpy as jnp
from jax import lax
import numpy as np

D_MODEL = 1024
BATCH = 8
SEQ = 4096
DEPTH = 1

HEAD_DIM = 64
MIX_WIDTH = D_MODEL
POOL_WIDTH = MIX_WIDTH // 4
POOL_GROUPS = 4
POOL_GROUP_DIM = POOL_WIDTH // POOL_GROUPS
POOL_WINDOWS = (2, 4, 8, 16)
FOX_WIDTH = MIX_WIDTH // 2
FOX_HEADS = FOX_WIDTH // HEAD_DIM
MEM_HEADS = 4
MEM_WIDTH = MEM_HEADS * HEAD_DIM
N_MEM = 256
Q_BLOCK = 128
EPS = 1e-6
SPLIT_SIZES = (POOL_WIDTH, POOL_WIDTH, FOX_WIDTH, FOX_WIDTH, FOX_WIDTH, FOX_HEADS, FOX_WIDTH, MEM_WIDTH, MEM_WIDTH)
IN_WIDTH = sum(SPLIT_SIZES)

kernel_name = "hymba_pool_fox_memory_layer"


def rms_norm(x, g):
    xf = x.astype(jnp.float32)
    y = xf * lax.rsqrt(jnp.mean(xf * xf, axis=-1, keepdims=True) + EPS)
    return (y * g.astype(jnp.float32)).astype(x.dtype)


def split_cols(proj):
    parts, off = [], 0
    for w in SPLIT_SIZES:
        parts.append(proj[..., off:off + w])
        off += w
    return parts


def to_heads(t, n_heads):
    b, s, _ = t.shape
    return t.reshape(b, s, n_heads, HEAD_DIM).transpose(0, 2, 1, 3)


def from_heads(t):
    b, h, s, d = t.shape
    return t.transpose(0, 2, 1, 3).reshape(b, s, h * d)


def pool_mixer(u, w_pool, scale):
    b, s, _ = u.shape
    uf = u.reshape(b, s, POOL_GROUPS, POOL_GROUP_DIM).astype(jnp.float32)
    csum = jnp.cumsum(uf, axis=1)
    pos = jnp.arange(1, s + 1, dtype=jnp.float32)
    pooled = []
    for g, w in enumerate(POOL_WINDOWS):
        cg = csum[:, :, g]
        lag = jnp.pad(cg, ((0, 0), (w, 0), (0, 0)))[:, :s]
        count = jnp.minimum(pos, float(w))
        pooled.append((cg - lag) / count[None, :, None])
    pooled = jnp.stack(pooled, axis=2)
    d = (pooled - uf).astype(u.dtype)
    y = jnp.einsum('bsgc,gce->bsge', d, w_pool).reshape(b, s, POOL_WIDTH)
    return y * scale


def fox_attention(q, k, v, logf):
    b, h, s, dh = q.shape
    n_blocks = s // Q_BLOCK
    F = jnp.cumsum(logf.astype(jnp.float32), axis=-1)
    qb = q.reshape(b, h, n_blocks, Q_BLOCK, dh).transpose(2, 0, 1, 3, 4)
    Fb = F.reshape(b, h, n_blocks, Q_BLOCK).transpose(2, 0, 1, 3)
    pos_k = jnp.arange(s)
    scale = 1.0 / math.sqrt(dh)

    def block(args):
        qi, Fi, i = args
        logits = jnp.einsum('bhqd,bhkd->bhqk', qi, k).astype(jnp.float32) * scale
        logits = logits + Fi[..., None] - F[:, :, None, :]
        pos_q = i * Q_BLOCK + jnp.arange(Q_BLOCK)
        causal = pos_k[None, :] <= pos_q[:, None]
        logits = jnp.where(causal, logits, -1e30)
        p = jax.nn.softmax(logits, axis=-1)
        return jnp.einsum('bhqk,bhkd->bhqd', p.astype(v.dtype), v)

    out = lax.map(block, (qb, Fb, jnp.arange(n_blocks)))
    return out.transpose(1, 2, 0, 3, 4).reshape(b, h, s, dh)


def memory_attention(q, k, v):
    scale = 1.0 / math.sqrt(q.shape[-1])
    logits = jnp.einsum('bhqd,bhmd->bhqm', q, k).astype(jnp.float32) * scale
    p = jax.nn.softmax(logits, axis=-1)
    return jnp.einsum('bhqm,bhmd->bhqd', p.astype(v.dtype), v)


def setup_inputs(seed: int = 0) -> dict:
    key = jax.random.key(seed)
    ks = jax.random.split(key, 16)
    f32 = jnp.float32
    x = jax.random.normal(ks[0], (BATCH, SEQ, D_MODEL), f32)
    mem = jax.random.normal(ks[1], (BATCH, N_MEM, D_MODEL), f32)
    norm_g = 1.0 + 0.02 * jax.random.normal(ks[2], (DEPTH, D_MODEL), f32)
    w_in = jax.random.normal(ks[3], (DEPTH, D_MODEL, IN_WIDTH), f32) * D_MODEL ** -0.5
    b_f = 1.0 + 4.0 * jax.random.uniform(ks[4], (DEPTH, FOX_HEADS), f32)
    w_pool = jax.random.normal(ks[5], (DEPTH, POOL_GROUPS, POOL_GROUP_DIM, POOL_GROUP_DIM), f32) * POOL_GROUP_DIM ** -0.5
    pool_scale = 1.0 + 0.02 * jax.random.normal(ks[6], (DEPTH, POOL_WIDTH), f32)
    fox_q_g = 1.0 + 0.02 * jax.random.normal(ks[7], (DEPTH, HEAD_DIM), f32)
    fox_k_g = 1.0 + 0.02 * jax.random.normal(ks[8], (DEPTH, HEAD_DIM), f32)
    mem_norm_g = 1.0 + 0.02 * jax.random.normal(ks[9], (DEPTH, D_MODEL), f32)
    w_mem_kv = jax.random.normal(ks[10], (DEPTH, D_MODEL, 2 * MEM_WIDTH), f32) * D_MODEL ** -0.5
    mem_q_g = 1.0 + 0.02 * jax.random.normal(ks[11], (DEPTH, HEAD_DIM), f32)
    mem_k_g = 1.0 + 0.02 * jax.random.normal(ks[12], (DEPTH, HEAD_DIM), f32)
    w_out = jax.random.normal(ks[13], (DEPTH, MIX_WIDTH, D_MODEL), f32) * MIX_WIDTH ** -0.5
    return {"x": x, "mem": mem, "norm_g": norm_g, "w_in": w_in, "b_f": b_f,
            "w_pool": w_pool, "pool_scale": pool_scale, "fox_q_g": fox_q_g,
            "fox_k_g": fox_k_g, "mem_norm_g": mem_norm_g, "w_mem_kv": w_mem_kv,
            "mem_q_g": mem_q_g, "mem_k_g": mem_k_g, "w_out": w_out}


def reference(x, mem, norm_g, w_in, b_f, w_pool, pool_scale, fox_q_g, fox_k_g,
              mem_norm_g, w_mem_kv, mem_q_g, mem_k_g, w_out):
    for l in range(DEPTH):
        h = rms_norm(x, norm_g[l])
        proj = jnp.einsum('bsd,de->bse', h, w_in[l])
        u_a, g_a, q_b, k_b, v_b, f_b, g_b, q_m, g_m = split_cols(proj)

        y_a = pool_mixer(u_a, w_pool[l], pool_scale[l])

        q = rms_norm(to_heads(q_b, FOX_HEADS), fox_q_g[l])
        k = rms_norm(to_heads(k_b, FOX_HEADS), fox_k_g[l])
        v = to_heads(v_b, FOX_HEADS)
        logf = jax.nn.log_sigmoid((f_b + b_f[l]).astype(jnp.float32)).transpose(0, 2, 1)
        y_b = from_heads(fox_attention(q, k, v, logf))

        mem_n = rms_norm(mem, mem_norm_g[l])
        kv = jnp.einsum('bmd,de->bme', mem_n, w_mem_kv[l])
        k_m = rms_norm(to_heads(kv[..., :MEM_WIDTH], MEM_HEADS), mem_k_g[l])
        v_m = to_heads(kv[..., MEM_WIDTH:], MEM_HEADS)
        q_mh = rms_norm(to_heads(q_m, MEM_HEADS), mem_q_g[l])
        y_m = from_heads(memory_attention(q_mh, k_m, v_m))

        mixed = jnp.concatenate([y_a * jax.nn.silu(g_a),
                                 y_b * jax.nn.silu(g_b),
                                 y_m * jax.nn.silu(g_m)], axis=-1)
        x = x + jnp.einsum('bse,ed->bsd', mixed, w_out[l])
    return x
```

```python
import numpy as np
from contextlib import ExitStack
import concourse.bass as bass
import concourse.mybir as mybir
from concourse.bass_utils import run_bass_kernel_spmd

F32 = mybir.dt.float32
BF16 = mybir.dt.bfloat16
AF = mybir.ActivationFunctionType
ALU = mybir.AluOpType

SEQ = 4096
D = 1024
NT = SEQ // 128
NB = SEQ // 512
EPS = 1e-6
NEG = -30000.0
C_UA, C_GA, C_Q, C_K, C_V, C_F, C_GB, C_QM, C_GM = 0, 256, 512, 1024, 1536, 2048, 2056, 2568, 2824
IN_W = 3080


_LAST = {}


class Sched:
    def __init__(self, nc, stack, needed=None):
        self.needed = needed
        self.needed_out = set()
        self.real = {}
        self.real_map = {}
        self.nc = nc
        self.eng = {"pe": nc.tensor, "act": nc.scalar, "dve": nc.vector, "pool": nc.gpsimd, "sp": nc.sync}
        self.sem = {k: stack.enter_context(nc.semaphore("s_" + k)) for k in self.eng}
        self.cnt = {k: 0 for k in self.eng}
        self.waited = {k: {} for k in self.eng}
        self.last_w = {}
        self.readers = {}
        self.dma_sems = {}
        self.stack = stack
        self.nwait = 0

    def _deps(self, reads, writes):
        deps = []
        for k in reads:
            if k in self.last_w:
                deps.append(self.last_w[k])
        for k in writes:
            if k in self.last_w:
                deps.append(self.last_w[k])
            deps.extend(self.readers.get(k, ()))
        return deps

    def _need(self, e, deps):
        need = {}
        for (sem, val, src) in deps:
            if src == "pe" and e == "pe":
                continue
            key = id(sem)
            if key not in need or need[key][1] < val:
                need[key] = (sem, val, src)
        out = []
        for key, (sem, val, src) in need.items():
            if self.waited[e].get(key, 0) >= val:
                continue
            self.waited[e][key] = val
            if src != "dma":
                en = src[:-2] if src.endswith("_b") else src
                self.needed_out.add((en, val))
                if self.needed is not None:
                    val = self.real_map[en][val]
            out.append((sem, val))
        return out

    def _wait(self, e, deps, ins_fn=None):
        ws = self._need(e, deps)
        if ins_fn is None:
            for (sem, val) in ws:
                self.eng[e].wait_ge(sem, val)
                self.nwait += 1
            return None
        for (sem, val) in ws[:-1]:
            self.eng[e].wait_ge(sem, val)
            self.nwait += 1
        ins = ins_fn()
        if ws:
            ins._wait_ge(ws[-1][0], ws[-1][1])
        return ins

    def _commit(self, ticket, reads, writes):
        for k in reads:
            self.readers.setdefault(k, []).append(ticket)
        for k in writes:
            self.last_w[k] = ticket
            self.readers[k] = []

    PSUM_KEYS = ("pA", "pAm", "pS", "pO", "pN", "pT")

    rec = None

    def record(self, f):
        assert self.rec is None
        self.rec = []
        f()
        r, self.rec = self.rec, None
        return r

    HOP = 0.25
    COST = {"pe": 0.3, "act": 0.65, "dve": 0.7, "pool": 1.2, "sp": 0.05}

    def emit_scheduled(self, lists):
        if not hasattr(self, "sim_eng"):
            self.sim_eng, self.sim_w, self.sim_r = {}, {}, {}
        pos = [0] * len(lists)

        def info(kind, args):
            if kind == "op":
                e, fn, reads, writes, cost = args
            else:
                e, out, in_, reads, writes, key, kw = args
                cost = None
            ps = [k for k in reads if (k[0] if isinstance(k, tuple) else k) in self.PSUM_KEYS]
            rd = [k for k in reads if k not in ps]
            wr = list(writes) + ps
            return e, rd, wr, cost

        while True:
            best = None
            for i, l in enumerate(lists):
                if pos[i] >= len(l):
                    continue
                kind, args = l[pos[i]]
                e, rd, wr, cost = info(kind, args)
                ready = 0.0
                for k in rd:
                    ready = max(ready, self.sim_w.get(k, 0.0))
                for k in wr:
                    ready = max(ready, self.sim_w.get(k, 0.0), self.sim_r.get(k, 0.0))
                start = max(ready + self.HOP, self.sim_eng.get(e, 0.0))
                cand = (start, pos[i] / len(l))
                if best is None or cand < best[0]:
                    best = (cand, i, start, e, rd, wr, cost, kind, args)
            if best is None:
                break
            _, i, start, e, rd, wr, cost, kind, args = best
            pos[i] += 1
            if kind == "op":
                dur = cost if cost is not None else self.COST[e]
                end = start + dur
                self.sim_eng[e] = end
                self.op(*args[:4])
            else:
                self.sim_eng[e] = start + 0.1
                end = start + 3.0
                e_, out, in_, reads, writes, key, kw = args
                self.dma(e_, out, in_, reads, writes, key, **kw)
            for k in rd:
                self.sim_r[k] = max(self.sim_r.get(k, 0.0), end)
            for k in wr:
                self.sim_w[k] = end

    def emit_interleaved(self, lists):
        pos = [0] * len(lists)
        while True:
            best, bf = None, 2.0
            for i, l in enumerate(lists):
                if pos[i] < len(l):
                    fr = pos[i] / len(l)
                    if fr < bf:
                        best, bf = i, fr
            if best is None:
                break
            kind, args = lists[best][pos[best]]
            pos[best] += 1
            if kind == "op":
                self.op(*args[:4])
            else:
                e, out, in_, reads, writes, key, kw = args
                self.dma(e, out, in_, reads, writes, key, **kw)

    def op(self, e, fn, reads=(), writes=(), cost=None):
        if self.rec is not None:
            self.rec.append(("op", (e, fn, list(reads), list(writes), cost)))
            return
        ps = [k for k in reads if (k[0] if isinstance(k, tuple) else k) in self.PSUM_KEYS]
        if ps:
            reads = [k for k in reads if k not in ps]
            writes = list(writes) + ps
        ins = self._wait(e, self._deps(reads, writes), ins_fn=lambda: fn(self.eng[e]))
        self.cnt[e] += 1
        if self.needed is None or (e, self.cnt[e]) in self.needed:
            self.real[e] = self.real.get(e, 0) + 1
            self.real_map.setdefault(e, {})[self.cnt[e]] = self.real[e]
            ins.then_inc(self.sem[e], 1)
        self._commit((self.sem[e], self.cnt[e], e), reads, writes)

    def dma(self, e, out, in_, reads=(), writes=(), key=None, **kw):
        if key is None:
            key = writes[0]
        if self.rec is not None:
            self.rec.append(("dma", (e, out, in_, list(reads), list(writes), key, kw)))
            return
        if key not in self.dma_sems:
            self.dma_sems[key] = [self.stack.enter_context(self.nc.semaphore("d_%d" % len(self.dma_sems))), 0]
        self._wait(e, self._deps(reads, writes))
        ds = self.dma_sems[key]
        ds[1] += 16
        self.eng[e].dma_start(out=out, in_=in_, **kw).then_inc(ds[0], 16)
        self._commit((ds[0], ds[1], "dma"), reads, writes)

    def barrier(self):
        allt = [(self.sem[k], self.cnt[k], k + "_b") for k in self.eng if self.cnt[k] > 0]
        allt += [(ds[0], ds[1], "dma") for ds in self.dma_sems.values() if ds[1] > 0]
        for e in self.eng:
            self._wait(e, allt)
        self.last_w = {}
        self.readers = {}


class Rot:
    def __init__(self, name, tensors):
        self.name, self.t, self.i = name, tensors, 0

    def next(self):
        j = self.i % len(self.t)
        self.i += 1
        return self.t[j], (self.name, j)


def build_nc(debug=False, stop=99, needed=None):
    nc = bass.Bass("TRN2", target_bir_lowering=False)
    dr = lambda n, s, k="ExternalInput": nc.dram_tensor(n, s, F32, kind=k).ap()
    x_d = dr("x", [SEQ, D])
    mem_d = dr("mem", [256, D])
    ng_d = dr("norm_g", [1, D])
    win_d = dr("w_in", [D, IN_W])
    bf_d = dr("b_f", [8, 1])
    wp_d = dr("w_pool", [4, 64, 64])
    ps_d = dr("pool_scale", [128, 2])
    fq_d = dr("fox_q_g", [128, 1])
    fk_d = dr("fox_k_g", [128, 1])
    mg_d = dr("mem_norm_g", [1, D])
    wkv_d = dr("w_mem_kv", [D, 512])
    mq_d = dr("mem_q_g", [128, 1])
    mk_d = dr("mem_k_g", [128, 1])
    wo_d = dr("w_out", [D, D])
    out_d = dr("out", [SEQ, D], "ExternalOutput")
    dbg_d = nc.dram_tensor("dbg", [128, 8 * SEQ], BF16, kind="ExternalOutput").ap() if debug else None

    with ExitStack() as gst:
        S = Sched(nc, gst, needed)
        _LAST["S"] = S
        cnt = [0]

        def sbt(st, shape, dt, name=None):
            cnt[0] += 1
            return st.enter_context(nc.sbuf_tensor(name or ("t%d" % cnt[0]), shape, dt))

        def pst(st, shape, dt, name=None):
            cnt[0] += 1
            return st.enter_context(nc.psum_tensor(name or ("p%d" % cnt[0]), shape, dt))

        ident_bf = sbt(gst, [128, 128], BF16)
        ident_f = sbt(gst, [128, 128], F32)
        bones = sbt(gst, [128, 128], BF16)
        trimask = sbt(gst, [128, 128], BF16)
        neghalf = sbt(gst, [128, 512], F32)
        zt = sbt(gst, [128, 128], F32)
        S.op("pool", lambda e: e.memset(zt[:], 0.0), writes=["zt"])
        S.op("pool", lambda e: e.affine_select(out=ident_f[:], in_=zt[:], pattern=[[-1, 128]], compare_op=ALU.not_equal,
                                               fill=1.0, base=0, channel_multiplier=1), reads=["zt"], writes=["ident_f"])
        S.op("pool", lambda e: e.tensor_copy(out=ident_bf[:], in_=ident_f[:]), reads=["ident_f"], writes=["ident_bf"])
        S.op("pool", lambda e: e.affine_select(out=trimask[:], in_=zt[:], pattern=[[1, 128]], compare_op=ALU.is_ge,
                                               fill=NEG, base=0, channel_multiplier=-1), reads=["zt"], writes=["trimask"])
        S.op("pool", lambda e: e.memset(bones[:], 0.0), writes=["bones"])
        S.op("pool", lambda e: e.memset(bones[0:64, 0:64], 1.0), reads=["bones"], writes=["bones"])
        S.op("pool", lambda e: e.memset(bones[64:128, 64:128], 1.0), reads=["bones"], writes=["bones"])
        S.op("pool", lambda e: e.memset(neghalf[:], -0.5), writes=["neghalf"])
        eps64 = sbt(gst, [128, 1], F32)
        S.op("pool", lambda e: e.memset(eps64[:], 64.0 * EPS), writes=["eps64"])

        gqk = sbt(gst, [128, 1], F32)
        gmqk = sbt(gst, [128, 1], F32)
        pscol = sbt(gst, [128, 2], F32)
        negb = sbt(gst, [8, 1], F32)
        tq = sbt(gst, [128, 4], F32)
        S.dma("sp", tq[:, 0:1], fq_d[:, :], writes=["tq0"])
        S.dma("sp", tq[:, 1:2], fk_d[:, :], writes=["tq1"])
        S.dma("sp", tq[:, 2:3], mq_d[:, :], writes=["tq2"])
        S.dma("sp", tq[:, 3:4], mk_d[:, :], writes=["tq3"])
        S.dma("sp", pscol[:], ps_d[:, :], writes=["pscol"])
        S.dma("sp", negb[:], bf_d[:, :], writes=["negb"])
        S.op("dve", lambda e: e.tensor_tensor(out=gqk[:], in0=tq[:, 0:1], in1=tq[:, 1:2], op=ALU.mult), reads=["tq0", "tq1"], writes=["gqk"])
        S.op("dve", lambda e: e.tensor_tensor(out=gmqk[:], in0=tq[:, 2:3], in1=tq[:, 3:4], op=ALU.mult), reads=["tq2", "tq3"], writes=["gmqk"])
        S.op("dve", lambda e: e.tensor_scalar(out=negb[:], in0=negb[:], scalar1=-1.0, scalar2=None, op0=ALU.mult), reads=["negb"], writes=["negb"])

        hT = sbt(gst, [128, 8, SEQ], BF16, "hT")
        kmT = sbt(gst, [64, 2, 2, 256], BF16, "kmT")
        Vm = sbt(gst, [128, 2, 2, 192], BF16, "Vm")
        R2c = sbt(gst, [128, 512], BF16, "R2c")
        Gk = sbt(gst, [128, 32, 8], F32, "Gk")
        cGb = sbt(gst, [128, 8, 8], F32, "cGb")
        wpbd = sbt(gst, [128, 2, 128], BF16, "wpbd")
        invw = sbt(gst, [128, 2], F32, "invw")
        invc = sbt(gst, [128, 2, 16], F32, "invc")

        for c in range(2):
            for hf in range(2):
                w = (2, 4, 8, 16)[2 * c + hf]
                S.op("pool", lambda e, c=c, hf=hf, w=w: e.memset(invw[hf * 64:(hf + 1) * 64, c:c + 1], 1.0 / w), writes=[("invw", c, hf)])
        iot = sbt(gst, [128, 16], F32)
        S.op("pool", lambda e: e.iota(iot[:], pattern=[[1, 16]], base=1, channel_multiplier=0, allow_small_or_imprecise_dtypes=True), writes=["iot"])
        for c in range(2):
            for hf in range(2):
                w = (2, 4, 8, 16)[2 * c + hf]
                S.op("dve", lambda e, c=c, hf=hf, w=w: e.tensor_scalar(out=invc[hf * 64:(hf + 1) * 64, c, :], in0=iot[hf * 64:(hf + 1) * 64, :],
                                                                     scalar1=float(w), scalar2=None, op0=ALU.min), reads=["iot"], writes=[("invc", c, hf)])
        S.op("dve", lambda e: e.reciprocal(out=invc[:], in_=invc[:]), reads=[("invc", c, hf) for c in range(2) for hf in range(2)], writes=["invc"])
        S.op("pool", lambda e: e.memset(wpbd[:], 0.0), writes=["wpbd"])
        for g in range(4):
            hf, c = g % 2, g // 2
            S.dma("pool", wpbd[hf * 64:(hf + 1) * 64, c, hf * 64:(hf + 1) * 64], wp_d[g, :, :], reads=["wpbd"], writes=[("wpbd", g)])
        wpbd_keys = ["wpbd"] + [("wpbd", g) for g in range(4)]

        def qk_norm(ps_ap, pskey, N, gscal, gkeys, dstA, dstB, dkeyA, dkeyB, sqr, msr, rsr, pN):
            sq, sqk = sqr.next()
            S.op("act", lambda e: e.activation(out=sq[:, 0:N], in_=ps_ap, func=AF.Square), reads=[pskey], writes=[sqk])
            S.op("pe", lambda e: e.matmul(pN[:, 0:N], lhsT=bones[:], rhs=sq[:, 0:N], start=True, stop=True), reads=[sqk, "bones"], writes=["pN"])
            ms, msk = msr.next()
            S.op("act", lambda e: e.activation(out=ms[:, 0:N], in_=pN[:, 0:N], func=AF.Ln, bias=eps64[:, 0:1], scale=1.0), reads=["pN", "eps64"], writes=[msk])
            rs, rsk = rsr.next()
            S.op("act", lambda e: e.activation(out=rs[:, 0:N], in_=ms[:, 0:N], func=AF.Exp, scale=-0.5), reads=[msk], writes=[rsk])
            for hf, dst, dk in ((0, dstA, dkeyA), (1, dstB, dkeyB)):
                sc = gscal if isinstance(gscal, float) else gscal[hf * 64:(hf + 1) * 64, 0:1]
                S.op("dve", lambda e, hf=hf, dst=dst, sc=sc: e.scalar_tensor_tensor(
                    out=dst, in0=ps_ap[hf * 64:(hf + 1) * 64, :], scalar=sc, in1=rs[hf * 64:(hf + 1) * 64, 0:N], op0=ALU.mult, op1=ALU.mult),
                    reads=[pskey, rsk] + list(gkeys), writes=[dk])

        def run_attention(tiles, pSr, PTr, pO):
            L = 2
            nt = len(tiles)
            st = [None] * nt
            cur = []
            rate = 0
            for n in range(nt + L):
                if n < nt:
                    t = tiles[n]
                    if t.get("pre"):
                        t["pre"]()
                    if t.get("drain"):
                        while cur:
                            cur.pop(0)()
                    if "bg" in t:
                        while cur:
                            cur.pop(0)()
                        cur = list(t["bg"]())
                        rate = -(-len(cur) // max(1, t["bg_tiles"] - 3))
                    ps, psk = pSr.next()
                    c0, c1 = t["c0"], 512
                    S.op("pe", lambda e, t=t, ps=ps: e.matmul(ps[:, c0:c1], lhsT=t["kT"], rhs=t["qT"], start=True, stop=not t["mask"]),
                         reads=t["kkeys"] + t["qkeys"], writes=[psk])
                    if t["mask"]:
                        S.op("pe", lambda e, ps=ps: e.matmul(ps[:, c0:c0 + 128], lhsT=ident_bf[:], rhs=trimask[:], start=False, stop=True),
                             reads=["ident_bf", "trimask"], writes=[psk])
                    pt, ptk = PTr.next()
                    if t["bias"] is not None:
                        S.op("act", lambda e, t=t, ps=ps, pt=pt: e.activation(out=pt[:, c0:c1], in_=ps[:, c0:c1], func=AF.Exp, bias=t["bias"], scale=1.0),
                             reads=[psk] + t["bkeys"], writes=[ptk])
                    else:
                        S.op("act", lambda e, ps=ps, pt=pt: e.activation(out=pt[:, c0:c1], in_=ps[:, c0:c1], func=AF.Exp), reads=[psk], writes=[ptk])
                    st[n] = (pt, ptk)
                m = n - L
                if m >= 0:
                    t = tiles[m]
                    pt, ptk = st[m]
                    c0, c1 = t["c0"], 512
                    ob = pO[t["ob"]]
                    S.op("pe", lambda e, t=t, pt=pt, ob=ob: e.matmul(ob[:, c0:c1], lhsT=t["v"], rhs=pt[:, c0:c1], start=t["ostart"], stop=t["olast"]),
                         reads=[ptk] + t["vkeys"], writes=[("pO", t["ob"])])
                    if t["olast"]:
                        t["fin"]()
                for _ in range(rate):
                    if cur:
                        cur.pop(0)()
            while cur:
                cur.pop(0)()

        def finalize(ob_ap, obkey, a, mixed_ap, mkey, rden, tmp, act_recip=False):
            lo, hi = a * 64, a * 64 + 64
            dlo, dhi = (64, 128) if a == 0 else (0, 64)
            if act_recip:
                S.op("act", lambda e: e.activation(out=rden[lo:hi, :], in_=ob_ap[dlo:dhi, :], func=AF.Ln), reads=[obkey], writes=[("rden", a)])
                S.op("act", lambda e: e.activation(out=rden[lo:hi, :], in_=rden[lo:hi, :], func=AF.Exp, scale=-1.0), reads=[("rden", a)], writes=[("rden", a)])
            else:
                S.op("dve", lambda e: e.reciprocal(out=rden[lo:hi, :], in_=ob_ap[dlo:dhi, :]), reads=[obkey], writes=[("rden", a)], cost=3.4)
            S.op("dve", lambda e: e.tensor_tensor(out=tmp[lo:hi, :], in0=ob_ap[lo:hi, :], in1=mixed_ap, op=ALU.mult), reads=[obkey, mkey], writes=[("tmpf", a)])
            S.op("pool", lambda e: e.tensor_tensor(out=mixed_ap, in0=tmp[lo:hi, :], in1=rden[lo:hi, :], op=ALU.mult),
                 reads=[("rden", a), ("tmpf", a)], writes=[mkey])

        mixA = sbt(gst, [128, 4, SEQ], BF16, "mixA")

        def make_norm_transpose(junk, hbr, ssr, pTr):
            evac_i = [0]

            def norm_transpose(src, srckey, gt, gtkey, dst3, dstkey):
                ss, ssk = ssr.next()
                hb, hbk = hbr.next()
                S.op("act", lambda e: e.activation(out=hb[:], in_=src, func=AF.Square, accum_out=ss[:, 0:1]), reads=[srckey], writes=[hbk, (ssk, 0)], cost=1.1)
                S.op("dve", lambda e: e.tensor_scalar(out=ss[:, 1:2], in0=ss[:, 0:1], scalar1=1.0 / D, scalar2=EPS, op0=ALU.mult, op1=ALU.add),
                     reads=[(ssk, 0)], writes=[(ssk, 1)], cost=0.15)
                S.op("pool", lambda e: e.tensor_tensor(out=ss[:, 2:3], in0=ss[:, 1:2], in1=neghalf[:, 0:1], op=ALU.pow), reads=[(ssk, 1), "neghalf"], writes=[(ssk, 2)], cost=0.5)
                S.op("dve", lambda e: e.scalar_tensor_tensor(out=hb[:], in0=src, scalar=ss[:, 2:3], in1=gt[:], op0=ALU.mult, op1=ALU.mult),
                     reads=[srckey, (ssk, 2), gtkey], writes=[hbk], cost=1.3)
                pT, pTk = pTr.next()
                for k in range(8):
                    S.op("pe", lambda e, k=k: e.transpose(out=pT[:, k * 128:(k + 1) * 128], in_=hb[:, k * 128:(k + 1) * 128], identity=ident_bf[:]),
                         reads=[hbk, "ident_bf"], writes=[pTk], cost=0.15)
                eng = "act" if evac_i[0] % 2 == 0 else "dve"
                evac_i[0] += 1
                src3 = pT[:].rearrange("p (k t) -> p k t", t=128)
                if eng == "act":
                    S.op("act", lambda e: e.copy(out=dst3, in_=src3), reads=[pTk], writes=[dstkey], cost=1.0)
                else:
                    S.op("dve", lambda e: e.tensor_copy(out=dst3, in_=src3), reads=[pTk], writes=[dstkey], cost=1.1)
            return norm_transpose

        with ExitStack() as pa_:
            gx = sbt(pa_, [128, D], F32)
            S.dma("sp", gx[:], ng_d[0:1, :].to_broadcast([128, D]), writes=["gx"])
            xr = Rot("xt", [sbt(pa_, [128, D], F32) for _ in range(2)])
            junk = None
            hbr = Rot("hb", [sbt(pa_, [128, D], BF16) for _ in range(2)])
            ssr = Rot("ss", [sbt(pa_, [128, 4], F32) for _ in range(2)])
            pTr = Rot("pT", [pst(pa_, [128, 8 * 128], BF16) for _ in range(1)])
            pAr = Rot("pA", [pst(pa_, [128, 512], F32)])
            pAm = Rot("pAm", [pst(pa_, [128, 512], F32)])
            pN = pst(pa_, [128, 512], F32)
            pSr = Rot("pS", [pst(pa_, [128, 512], F32) for _ in range(2)])
            pO = [pst(pa_, [128, 512], F32) for _ in range(2)]
            sqr = Rot("sq", [sbt(pa_, [128, 512], BF16) for _ in range(2)])
            msr = Rot("ms", [sbt(pa_, [128, 512], F32)])
            rsr = Rot("rs", [sbt(pa_, [128, 512], F32)])
            PTr = Rot("PT", [sbt(pa_, [128, 512], BF16) for _ in range(3)])
            rden = sbt(pa_, [128, 512], F32)
            tmpf = sbt(pa_, [128, 512], F32)
            QAmb = [sbt(pa_, [64, 2, 2, 512], BF16) for _ in range(2)]
            U = sbt(pa_, [128, 528], F32)
            S2 = sbt(pa_, [128, 528], F32)
            S4 = sbt(pa_, [128, 528], F32)
            S8 = sbt(pa_, [128, 528], F32)
            Uh = sbt(pa_, [128, 2, 16], F32)
            t16 = sbt(pa_, [128, 16], F32)
            Dr = Rot("dd", [sbt(pa_, [128, 512], BF16) for _ in range(2)])
            norm_transpose = make_norm_transpose(junk, hbr, ssr, pTr)
            memT = sbt(pa_, [128, 8, 256], BF16, "memT")
            gm = sbt(pa_, [128, D], F32)
            xm = sbt(pa_, [128, 2, D], F32)
            wkv = sbt(pa_, [128, 8, 512], BF16)
            S.dma("sp", gm[:], mg_d[0:1, :].to_broadcast([128, D]), writes=["gm"])
            S.dma("sp", xm[:], mem_d.rearrange("(a p) d -> p a d", p=128), writes=["xm"])
            S.dma("pool", wkv[:], wkv_d.rearrange("(k p) c -> p k c", p=128), writes=["wkv"])
            for a in range(2):
                norm_transpose(xm[:, a, :], "xm", gm, "gm", memT[:, :, a * 128:(a + 1) * 128], ("memT", a))
            memk = [("memT", 0), ("memT", 1)]

            def mem_kv():
                S.op("pool", lambda e: e.memset(Vm[:], 1.0), writes=["Vm1"])
                for c in range(2):
                    pa, pak = pAm.next()
                    for k in range(8):
                        S.op("pe", lambda e, k=k, c=c, pa=pa: e.matmul(pa[:, 0:256], lhsT=wkv[:, k, c * 128:(c + 1) * 128], rhs=memT[:, k, :], start=(k == 0), stop=(k == 7)),
                             reads=["wkv"] + memk, writes=[pak])
                    qk_norm(pa[:, 0:256], pak, 256, 8.0, [], kmT[:, c, 0, :], kmT[:, c, 1, :], ("kmT", c, 0), ("kmT", c, 1), sqr, msr, rsr, pN)
                for J in range(2):
                    pa, pak = pAm.next()
                    for k in range(8):
                        S.op("pe", lambda e, k=k, J=J, pa=pa: e.matmul(pa[:, 0:256], lhsT=memT[:, k, J * 128:(J + 1) * 128], rhs=wkv[:, k, 256:512], start=(k == 0), stop=(k == 7)),
                             reads=["wkv"] + memk, writes=[pak])
                    src = pa[:, 0:256].rearrange("p (c a d) -> p c a d", c=2, a=2)
                    S.op("dve", lambda e, J=J, src=src: e.tensor_copy(out=Vm[:, J, :, 0:64], in_=src[:, :, 0, :]), reads=[pak, "Vm1"], writes=[("Vm", J, 0)])
                    S.op("dve", lambda e, J=J, src=src: e.tensor_copy(out=Vm[:, J, :, 128:192], in_=src[:, :, 1, :]), reads=[pak, "Vm1"], writes=[("Vm", J, 1)])
            wa = []
            for c0 in (C_UA, C_UA + 128, C_GA, C_GA + 128, C_QM, C_QM + 128, C_GM, C_GM + 128):
                w = sbt(pa_, [128, 8, 128], BF16)
                S.dma("pool", w[:], win_d[:, c0:c0 + 128].rearrange("(k p) c -> p k c", p=128), writes=[("wa", c0)])
                wa.append((w, ("wa", c0)))
            w_u, w_g, w_qm, w_gm = wa[0:2], wa[2:4], wa[4:6], wa[6:8]
            S.op("pool", lambda e: e.memset(Uh[:], 0.0), writes=["Uh"])

            def hk(b):
                return [("hT", 4 * b + j) for j in range(4)]

            def proj_a(w, wk, b, pa, pak):
                for k in range(8):
                    S.op("pe", lambda e, k=k: e.matmul(pa[:, :], lhsT=w[:, k, :], rhs=hT[:, k, b * 512:(b + 1) * 512], start=(k == 0), stop=(k == 7)),
                         reads=[wk] + hk(b), writes=[pak])

            def x_block(b):
                for j in range(4):
                    i = 4 * b + j
                    xt, xk = xr.next()
                    S.dma("sp", xt[:], x_d[i * 128:(i + 1) * 128, :], writes=[xk])
                    norm_transpose(xt[:], xk, gx, "gx", hT[:, :, i * 128:(i + 1) * 128], ("hT", i))

            def pool_block(b):
                blk = slice(b * 512, (b + 1) * 512)
                for c in range(2):
                    pa, pak = pAr.next()
                    proj_a(w_u[c][0], w_u[c][1], b, pa, pak)
                    S.op("act", lambda e, pa=pa: e.copy(out=U[:, 16:528], in_=pa[:, :]), reads=[pak], writes=["U"])
                    S.op("pool", lambda e, c=c: e.tensor_copy(out=U[:, 0:16], in_=Uh[:, c, :]), reads=["Uh", "U"], writes=["U"])
                    S.op("pool", lambda e: e.tensor_tensor(out=S2[:, 1:528], in0=U[:, 1:528], in1=U[:, 0:527], op=ALU.add), reads=["U"], writes=["S2"])
                    S.op("pool", lambda e: e.tensor_tensor(out=S4[:, 3:528], in0=S2[:, 3:528], in1=S2[:, 1:526], op=ALU.add), reads=["S2"], writes=["S4"])
                    if c == 1:
                        S.op("pool", lambda e: e.tensor_tensor(out=S8[:, 7:528], in0=S4[:, 7:528], in1=S4[:, 3:524], op=ALU.add), reads=["S4"], writes=["S8"])
                        S.op("pool", lambda e: e.tensor_tensor(out=S2[64:128, 15:528], in0=S8[64:128, 15:528], in1=S8[64:128, 7:520], op=ALU.add),
                             reads=["S8", "S4", "S2"], writes=["S2"])
                    srcs = ((0, S2), (1, S4)) if c == 0 else ((0, S8), (1, S2))
                    dd, ddk = Dr.next()
                    for (hf, Wt) in srcs:
                        sl = slice(hf * 64, (hf + 1) * 64)
                        S.op("dve", lambda e, c=c, sl=sl, Wt=Wt: e.scalar_tensor_tensor(out=dd[sl, :], in0=Wt[sl, 16:528], scalar=invw[sl, c:c + 1],
                                                                                        in1=U[sl, 16:528], op0=ALU.mult, op1=ALU.subtract),
                             reads=["U", "S2", "S4", "S8", ("invw", c, hf)], writes=[(ddk, hf)])
                        if b == 0:
                            S.op("dve", lambda e, c=c, sl=sl, Wt=Wt: e.tensor_tensor(out=t16[sl, :], in0=Wt[sl, 16:32], in1=invc[sl, c, :], op=ALU.mult),
                                 reads=["S2", "S4", "S8", "invc"], writes=[("t16", hf)])
                            S.op("dve", lambda e, sl=sl: e.tensor_tensor(out=dd[sl, 0:16], in0=t16[sl, :], in1=U[sl, 16:32], op=ALU.subtract),
                                 reads=[("t16", hf), "U", (ddk, hf)], writes=[(ddk, hf)])
                    S.op("pool", lambda e, c=c: e.tensor_copy(out=Uh[:, c, :], in_=U[:, 512:528]), reads=["U", "Uh"], writes=["Uh"])
                    pg, pgk = pAr.next()
                    proj_a(w_g[c][0], w_g[c][1], b, pg, pgk)
                    S.op("act", lambda e, c=c, pg=pg: e.activation(out=mixA[:, c, blk], in_=pg[:, :], func=AF.Silu), reads=[pgk], writes=[("mx", c, b)])
                    py, pyk = pAr.next()
                    S.op("pe", lambda e, c=c, py=py: e.matmul(py[:, :], lhsT=wpbd[:, c, :], rhs=dd[:, :], start=True, stop=True),
                         reads=[(ddk, 0), (ddk, 1)] + wpbd_keys, writes=[pyk])
                    S.op("dve", lambda e, c=c, py=py: e.scalar_tensor_tensor(out=mixA[:, c, blk], in0=py[:, :], scalar=pscol[:, c:c + 1], in1=mixA[:, c, blk],
                                                                             op0=ALU.mult, op1=ALU.mult), reads=[pyk, ("mx", c, b), "pscol", "U"], writes=[("mx", c, b)])

            def mem_proj(b):
                blk = slice(b * 512, (b + 1) * 512)
                QAm = QAmb[b % 2]
                for c in range(2):
                    pa, pak = pAm.next()
                    proj_a(w_qm[c][0], w_qm[c][1], b, pa, pak)
                    qk_norm(pa[:, :], pak, 512, gmqk, ["gmqk"], QAm[:, c, 0, :], QAm[:, c, 1, :], ("QAm", b % 2, c, 0), ("QAm", b % 2, c, 1), sqr, msr, rsr, pN)
                for c in range(2):
                    pg, pgk = pAm.next()
                    proj_a(w_gm[c][0], w_gm[c][1], b, pg, pgk)
                    S.op("act", lambda e, c=c, pg=pg: e.activation(out=mixA[:, 2 + c, blk], in_=pg[:, :], func=AF.Silu), reads=[pgk],
                         writes=[("mx", 6 + c, b, 0), ("mx", 6 + c, b, 1)])

            def mem_attn(b):
                blk = slice(b * 512, (b + 1) * 512)
                QAm = QAmb[b % 2]
                tiles = []
                for c in range(2):
                    for a in range(2):
                        oi = ocount_m[0] % 2
                        ocount_m[0] += 1
                        for J in range(2):
                            t = dict(kT=kmT[0:64, c, a, J * 128:(J + 1) * 128], kkeys=[("kmT", c, a)], qT=QAm[:, c, a, :], qkeys=[("QAm", b % 2, c, a)],
                                     c0=0, mask=False, bias=None, bkeys=[], v=Vm[:, J, c, a * 64:a * 64 + 128], vkeys=["Vm1", ("Vm", J, 0), ("Vm", J, 1)],
                                     ob=oi, ostart=(J == 0), olast=(J == 1))
                            if J == 1:
                                t["fin"] = (lambda a=a, c=c, oi=oi: finalize(pO[oi], ("pO", oi), a, mixA[a * 64:(a + 1) * 64, 2 + c, blk],
                                                                             ("mx", 6 + c, b, a), rden, tmpf, act_recip=True))
                            tiles.append(t)
                run_attention(tiles, pSr, PTr, pO)

            ocount_m = [0]
            for step in range(NB + 3):
                lists = []
                if step < NB:
                    lists.append(S.record(lambda: x_block(step)))
                if step == 0:
                    lists.append(S.record(mem_kv))
                if 1 <= step <= NB:
                    lists.append(S.record(lambda: pool_block(step - 1)))
                    lists.append(S.record(lambda: mem_proj(step - 1)))
                if 2 <= step <= NB + 1:
                    lists.append(S.record(lambda: mem_attn(step - 2)))
                S.emit_scheduled(lists)
            S.barrier()

        late = ExitStack()
        mixB = sbt(late, [128, 4, SEQ], BF16, "mixB")
        with ExitStack() as pg_:
            pA = [pst(pg_, [128, 512], F32) for _ in range(2)]
            wf = sbt(pg_, [128, 8, 8], BF16)
            with nc.allow_non_contiguous_dma("tiny f-gate weight slice"):
                S.dma("pool", wf[:], win_d[:, C_F:C_F + 8].rearrange("(k p) c -> p k c", p=128), writes=["wf"])
            onesf = sbt(pg_, [8, 1], F32)
            S.op("pool", lambda e: e.memset(onesf[:], 1.0), writes=["onesf"])
            A = sbt(pg_, [8, SEQ], F32)
            G = sbt(pg_, [8, SEQ], F32)
            Rhi = sbt(pg_, [8, SEQ], BF16)
            Rlo = sbt(pg_, [8, SEQ], BF16)
            Crep = sbt(pg_, [8, NB, 128], F32)
            pAgs = [pst(pg_, [128, 512], F32) for _ in range(3)]
            wgs = []
            for p in range(4):
                w = sbt(pg_, [128, 8, 128], BF16)
                S.dma("pool", w[:], win_d[:, C_GB + 128 * p:C_GB + 128 * (p + 1)].rearrange("(k p) c -> p k c", p=128), writes=[("wg", p)])
                wgs.append(w)

            def gates_list():
                gi_ = 0
                for b in range(NB):
                    for p in range(4):
                        blk = slice(b * 512, (b + 1) * 512)
                        pAg = pAgs[gi_ % 3]
                        pk = ("pAm", gi_ % 3)
                        gi_ += 1
                        for k in range(8):
                            S.op("pe", lambda e, k=k, p=p, b=b, pAg=pAg: e.matmul(pAg[:, :], lhsT=wgs[p][:, k, :], rhs=hT[:, k, b * 512:(b + 1) * 512], start=(k == 0), stop=(k == 7)),
                                 reads=[("wg", p)], writes=[pk], cost=0.25)
                        S.op("act", lambda e, p=p, blk=blk, pAg=pAg: e.activation(out=mixB[:, p, blk], in_=pAg[:, :], func=AF.Silu), reads=[pk],
                             writes=[("mx", 2 + p, b, 0), ("mx", 2 + p, b, 1)])
            def gchain():
                for b in range(NB):
                    pa = pA[b % 2]
                    for k in range(8):
                        S.op("pe", lambda e, k=k, b=b, pa=pa: e.matmul(pa[0:8, :], lhsT=wf[:, k, :], rhs=hT[:, k, b * 512:(b + 1) * 512], start=(k == 0), stop=(k == 7)),
                             reads=["wf"], writes=[("pA", b % 2)])
                    S.op("act", lambda e, b=b, pa=pa: e.activation(out=A[:, b * 512:(b + 1) * 512], in_=pa[0:8, :], func=AF.Exp, bias=negb[:, 0:1], scale=-1.0),
                         reads=[("pA", b % 2), "negb"], writes=[("A", b)])
                S.op("act", lambda e: e.activation(out=A[:], in_=A[:], func=AF.Ln, bias=1.0, scale=1.0), reads=[("A", b) for b in range(NB)], writes=["A"])
                S.op("dve", lambda e: e.tensor_tensor_scan(out=G[:], data0=onesf[:, 0:1].to_broadcast([8, SEQ]), data1=A[:], initial=0.0, op0=ALU.mult, op1=ALU.add),
                     reads=["A", "onesf"], writes=["G"])
                G3 = G[:].rearrange("p (i t) -> p i t", t=512)
                A3 = A[:].rearrange("p (i t) -> p i t", t=512)
                S.op("dve", lambda e: e.tensor_tensor(out=A3, in0=G3[:, :, 0:1].to_broadcast([8, NB, 512]), in1=G3, op=ALU.subtract), reads=["G", "A"], writes=["A"])
                S.op("dve", lambda e: e.tensor_copy(out=Rhi[:], in_=A[:]), reads=["A"], writes=["Rhi"])
                S.op("dve", lambda e: e.tensor_tensor(out=Rlo[:], in0=A[:], in1=Rhi[:], op=ALU.subtract), reads=["A", "Rhi"], writes=["Rlo"])
                for I in range(NB):
                    for j, (Rt, rk) in enumerate(((Rhi, "Rhi"), (Rlo, "Rlo"))):
                        pp = I * 16 + j * 8
                        S.dma("sp", R2c[pp:pp + 8, :], Rt[0:8, I * 512:(I + 1) * 512], reads=[rk], writes=[("R2c", I, j)], key=("R2c", (I * 2 + j) % 4))
            def gchain_b():
                G3 = G[:].rearrange("p (i t) -> p i t", t=512)
                pG = pA[0]
                for J in range(32):
                    S.op("pe", lambda e, J=J: e.transpose(out=pG[:, J * 8:(J + 1) * 8], in_=G[0:8, J * 128:(J + 1) * 128], identity=ident_f[0:8, 0:8]),
                         reads=["G", "ident_f", ("A", 7)], writes=[("pA", 0)])
                S.op("dve", lambda e: e.tensor_copy(out=Gk[:].rearrange("p j h -> p (j h)"), in_=pG[:, 0:256]), reads=[("pA", 0)], writes=["Gk"])
                S.op("dve", lambda e: e.tensor_copy(out=Crep[:], in_=G3[:, :, 0:1].to_broadcast([8, NB, 128])), reads=["G"], writes=["Crep"])
                pC = pA[1]
                for I in range(NB):
                    S.op("pe", lambda e, I=I: e.transpose(out=pC[:, I * 8:(I + 1) * 8], in_=Crep[:, I, :], identity=ident_f[0:8, 0:8]),
                         reads=["Crep", "ident_f", ("A", 7)], writes=[("pA", 1)])
                S.op("dve", lambda e: e.tensor_copy(out=cGb[:].rearrange("p i h -> p (i h)"), in_=pC[:, 0:64]), reads=[("pA", 1)], writes=["cGb"])

            gchain()
            gates_list()
            gchain_b()
            S.barrier()

        with ExitStack() as p2w:
            pAr = Rot("pA", [pst(p2w, [128, 512], F32) for _ in range(2)])
            pN = pst(p2w, [128, 512], F32)
            pSr = Rot("pS", [pst(p2w, [128, 512], F32) for _ in range(3)])
            pO = [pst(p2w, [128, 512], F32) for _ in range(2)]
            sqr = Rot("sq", [sbt(p2w, [128, 512], BF16) for _ in range(2)])
            msr = Rot("ms", [sbt(p2w, [128, 512], F32)])
            rsr = Rot("rs", [sbt(p2w, [128, 512], F32)])
            wr = Rot("w", [sbt(p2w, [128, 8, 128], BF16) for _ in range(6)])
            PTr = Rot("PT", [sbt(p2w, [128, 512], BF16) for _ in range(4)])
            rden = sbt(p2w, [128, 512], F32)
            tmpf = sbt(p2w, [128, 512], F32)
            QA = [sbt(p2w, [66, 2, 512], BF16) for _ in range(2)]
            gcount = [0]
            ocount = [0]

            def load_w(c0):
                w, wk = wr.next()
                S.dma("pool", w[:], win_d[:, c0:c0 + 128].rearrange("(k p) c -> p k c", p=128), writes=[wk])
                return w, wk

            def proj_fm(w, wk, b, pa, pak):
                for k in range(8):
                    S.op("pe", lambda e, k=k: e.matmul(pa[:, :], lhsT=w[:, k, :], rhs=hT[:, k, b * 512:(b + 1) * 512], start=(k == 0), stop=(k == 7)),
                         reads=[wk], writes=[pak])

            KA = sbt(p2w, [66, 2, SEQ], BF16, "KA")
            Vp = sbt(p2w, [128, 32, 192], BF16, "Vp")
            bkb = [sbt(p2w, [128, 2, 32, 8], F32) for _ in range(2)]
            W = {}

            def load_pair_w(p):
                wk_ = load_w(C_K + 128 * p)
                wq_ = load_w(C_Q + 128 * p)
                wv_ = load_w(C_V + 128 * p)
                W[p] = (wq_, wk_, wv_)

            def proj_thunks(w, wk, b, ctx, per=2):
                th = []

                def first():
                    ctx["pa"], ctx["pak"] = pAr.next()
                th.append(first)
                for k0 in range(0, 8, per):
                    def f(k0=k0):
                        pa, pak = ctx["pa"], ctx["pak"]
                        for k in range(k0, k0 + per):
                            S.op("pe", lambda e, k=k: e.matmul(pa[:, :], lhsT=w[:, k, :], rhs=hT[:, k, b * 512:(b + 1) * 512], start=(k == 0), stop=(k == 7)),
                                 reads=[wk], writes=[pak])
                    th.append(f)
                return th

            def norm_thunks(ctx, gscal, gkeys, dstA, dstB, dkeyA, dkeyB):
                N = 512
                th = []

                def f1():
                    ctx["sq"], ctx["sqk"] = sqr.next()
                    S.op("act", lambda e: e.activation(out=ctx["sq"][:, 0:N], in_=ctx["pa"][:, :], func=AF.Square), reads=[ctx["pak"]], writes=[ctx["sqk"]])

                def f2():
                    S.op("pe", lambda e: e.matmul(pN[:, 0:N], lhsT=bones[:], rhs=ctx["sq"][:, 0:N], start=True, stop=True), reads=[ctx["sqk"], "bones"], writes=["pN"])

                def f3():
                    ctx["ms"], ctx["msk"] = msr.next()
                    S.op("act", lambda e: e.activation(out=ctx["ms"][:, 0:N], in_=pN[:, 0:N], func=AF.Ln, bias=eps64[:, 0:1], scale=1.0), reads=["pN", "eps64"], writes=[ctx["msk"]])

                def f4():
                    ctx["rs"], ctx["rsk"] = rsr.next()
                    S.op("act", lambda e: e.activation(out=ctx["rs"][:, 0:N], in_=ctx["ms"][:, 0:N], func=AF.Exp, scale=-0.5), reads=[ctx["msk"]], writes=[ctx["rsk"]])

                def f5(hf, dst, dk):
                    sc = gscal if isinstance(gscal, float) else gscal[hf * 64:(hf + 1) * 64, 0:1]
                    S.op("dve", lambda e: e.scalar_tensor_tensor(out=dst, in0=ctx["pa"][hf * 64:(hf + 1) * 64, :], scalar=sc, in1=ctx["rs"][hf * 64:(hf + 1) * 64, 0:N],
                                                                 op0=ALU.mult, op1=ALU.mult), reads=[ctx["pak"], ctx["rsk"]] + list(gkeys), writes=[dk])
                th += [f1, f2, f3, f4, lambda: f5(0, dstA, dkeyA), lambda: f5(1, dstB, dkeyB)]
                return th

            def block_thunks(p, I, gi):
                th = []
                blk = slice(I * 512, (I + 1) * 512)
                bk = bkb[p % 2]
                if I == 0:
                    def fb():
                        for a in range(2):
                            h = 2 * p + a
                            S.op("dve", lambda e, a=a, h=h: e.tensor_tensor(out=bk[:, a, :, :], in0=Gk[:, :, h].unsqueeze(2).to_broadcast([128, 32, 8]),
                                                                            in1=cGb[:, :, h].unsqueeze(1).to_broadcast([128, 32, 8]), op=ALU.subtract),
                                 reads=[], writes=[("bk", p % 2, a)])
                    th.append(fb)
                if I == 1 and p + 1 < 4:
                    th.append(lambda: load_pair_w(p + 1))
                if p == 1:
                    th.append(lambda: S.dma("sp", out_d[I * 512:(I + 1) * 512, :], x_d[I * 512:(I + 1) * 512, :], writes=[("outinit", I)], key=("oinit", I % 2)))
                ck = {}
                th += proj_thunks(W[p][1][0], W[p][1][1], I, ck) if p in W else []
                cq = {}
                qa = QA[gi % 2]
                th += proj_thunks(W[p][0][0], W[p][0][1], I, cq)
                th += norm_thunks(ck, 8.0, [], KA[0:64, 0, blk], KA[0:64, 1, blk], ("KA", 0, I), ("KA", 1, I))
                th += norm_thunks(cq, gqk, ["gqk"], qa[0:64, 0, :], qa[0:64, 1, :], ("QA", gi % 2, 0), ("QA", gi % 2, 1))

                def fr():
                    for a in range(2):
                        h = 2 * p + a
                        for j in range(2):
                            pp = I * 16 + j * 8 + h
                            S.dma("sp", qa[64 + j:65 + j, a, :], R2c[pp:pp + 1, :], reads=[], writes=[("QAr", gi % 2, a, j)], key=("QAr", gi % 2, a, j))
                th.append(fr)
                cv = {}

                def v0():
                    cv["pa"], cv["pak"] = pAr.next()
                th.append(v0)
                for j in range(4):
                    def fv(j=j):
                        i = 4 * I + j
                        pa, pak = cv["pa"], cv["pak"]
                        wv = W[p][2]
                        for k in range(8):
                            S.op("pe", lambda e, k=k: e.matmul(pa[:, j * 128:(j + 1) * 128], lhsT=hT[:, k, i * 128:(i + 1) * 128], rhs=wv[0][:, k, :],
                                                               start=(k == 0), stop=(k == 7)), reads=[wv[1]], writes=[pak])
                    th.append(fv)

                def fve(q):
                    pa, pak = cv["pa"], cv["pak"]
                    src = pa[:, :].rearrange("p (j a d) -> p j a d", j=4, a=2)
                    S.op("dve", lambda e: e.tensor_copy(out=Vp[:, 4 * I:4 * I + 4, 128 * q:128 * q + 64], in_=src[:, :, q, :]), reads=[pak], writes=[("Vp", I, q)])
                th += [lambda: fve(0), lambda: fve(1)]
                return th

            load_pair_w(0)
            S.op("dve", lambda e: e.memset(KA[64:66, :, :], 1.0), writes=["KAones"])
            S.op("pool", lambda e: e.memset(Vp[:, :, 64:128], 1.0), writes=["Vpones"])
            blocks = [(p, I) for p in range(4) for I in range(NB)]
            gbase = gcount[0]
            tiles = []
            for bi, (p, I) in enumerate(blocks):
                gi = gbase + bi
                qa = QA[gi % 2]
                bk = bkb[p % 2]
                blk = slice(I * 512, (I + 1) * 512)
                for a in range(2):
                    oi = ocount[0] % 2
                    ocount[0] += 1
                    nJ = 4 * I + 4
                    for J in range(nJ):
                        diag = J >= 4 * I
                        c0 = 128 * (J - 4 * I) if diag else 0
                        t = dict(kT=KA[0:66, a, J * 128:(J + 1) * 128], kkeys=[("KA", a, J // 4), "KAones"], qT=qa[0:66, a, c0:512],
                                 qkeys=[("QA", gi % 2, a), ("QAr", gi % 2, a, 0), ("QAr", gi % 2, a, 1)], c0=c0, mask=diag, bias=bk[:, a, J, I:I + 1], bkeys=[("bk", p % 2, a)],
                                 v=Vp[:, J, a * 64:a * 64 + 128], vkeys=[("Vp", J // 4, 0), ("Vp", J // 4, 1), "Vpones"], ob=oi, ostart=(J == 0), olast=(J == nJ - 1))
                        if J == nJ - 1:
                            t["fin"] = (lambda oi=oi, a=a, p=p, I=I, blk=blk: finalize(pO[oi], ("pO", oi), a, mixB[a * 64:(a + 1) * 64, p, blk],
                                                                                       ("mx", 2 + p, I, a), rden, tmpf))
                        if a == 0 and J == 0:
                            t["drain"] = True
                        if bi + 1 < len(blocks):
                            np_, nI = blocks[bi + 1]
                            if np_ == p:
                                if a == 0 and J == 0:
                                    t["bg"] = (lambda np_=np_, nI=nI, gi=gi: block_thunks(np_, nI, gi + 1))
                                    t["bg_tiles"] = 2 * nJ
                            else:
                                if a == 1 and J == 4:
                                    t["bg"] = (lambda np_=np_, nI=nI, gi=gi: block_thunks(np_, nI, gi + 1))
                                    t["bg_tiles"] = nJ - 4
                        tiles.append(t)
            gcount[0] += len(blocks)
            for f in block_thunks(0, 0, gbase):
                f()
            run_attention(tiles, pSr, PTr, pO)
            S.barrier()
        def mxk(k):
            return mixA[:, k, :] if k < 2 else (mixB[:, k - 2, :] if k < 6 else mixA[:, k - 4, :])

        with ExitStack() as p3:
            pAr = Rot("pA", [pst(p3, [128, 512], F32) for _ in range(4)])
            wo = sbt(p3, [128, 8, D], BF16, "wo")
            for k in range(8):
                S.dma("pool", wo[:, k, :], wo_d[k * 128:(k + 1) * 128, :], writes=[("wo", k)])
            o3 = Rot("o3", [sbt(p3, [128, 2, D], F32) for _ in range(3)])
            for i2 in range(NT // 2):
                ot, ok = o3.next()
                first = {}
                if i2 == 0:
                    banks = [pAr.next() for _ in range(4)]
                    for k in range(8):
                        for q, (pa, pak) in enumerate(banks):
                            a_, n_ = q // 2, q % 2
                            S.op("pe", lambda e, k=k, n_=n_, pa=pa, a_=a_: e.matmul(pa[:, :], lhsT=mxk(k)[:, a_ * 128:(a_ + 1) * 128], rhs=wo[:, k, n_ * 512:(n_ + 1) * 512],
                                                                                   start=(k == 0), stop=(k == 7)), reads=[("wo", k)], writes=[pak])
                    for q, b_ in enumerate(banks):
                        first[(q // 2, q % 2)] = b_
                for a in range(2):
                    i = 2 * i2 + a
                    for n in range(2):
                        if (a, n) in first:
                            pa, pak = first[(a, n)]
                        else:
                            pa, pak = pAr.next()
                            for k in range(8):
                                S.op("pe", lambda e, k=k, n=n, pa=pa, i=i: e.matmul(pa[:, :], lhsT=mxk(k)[:, i * 128:(i + 1) * 128], rhs=wo[:, k, n * 512:(n + 1) * 512],
                                                                                   start=(k == 0), stop=(k == 7)), reads=[("wo", k)], writes=[pak])
                        if n == 0:
                            S.op("act", lambda e, n=n, pa=pa, ot=ot, a=a: e.copy(out=ot[:, a, n * 512:(n + 1) * 512], in_=pa[:, :]), reads=[pak], writes=[(ok, a, n)])
                        else:
                            S.op("dve", lambda e, n=n, pa=pa, ot=ot, a=a: e.tensor_copy(out=ot[:, a, n * 512:(n + 1) * 512], in_=pa[:, :]), reads=[pak], writes=[(ok, a, n)])
                    S.dma("pool", out_d[i * 128:(i + 1) * 128, :], ot[:, a, :], reads=[(ok, a, 0), (ok, a, 1)],
                          writes=[("out", i)], key=("odma", i2 % 3, a), accum_op=ALU.add)
            S.barrier()
        late.close()
    return nc


_NC_CACHE = {}


def build_two_pass():
    build_nc(False)
    needed = set(_LAST["S"].needed_out)
    return build_nc(False, needed=needed)


def kernel(x, mem, norm_g, w_in, b_f, w_pool, pool_scale, fox_q_g, fox_k_g, mem_norm_g, w_mem_kv, mem_q_g, mem_k_g, w_out):
    f = lambda a: np.ascontiguousarray(np.asarray(a, dtype=np.float32))
    x = f(x)
    mem = f(mem)
    B = x.shape[0]
    assert B == 8
    if "nc" not in _NC_CACHE:
        _NC_CACHE["nc"] = build_two_pass()
    nc = _NC_CACHE["nc"]
    rep2 = lambda v: f(np.concatenate([np.asarray(v).reshape(64), np.asarray(v).reshape(64)]).reshape(128, 1))
    shared = {
        "norm_g": f(np.asarray(norm_g).reshape(1, D)),
        "w_in": f(np.asarray(w_in).reshape(D, IN_W)),
        "b_f": f(np.asarray(b_f).reshape(8, 1)),
        "w_pool": f(np.asarray(w_pool).reshape(4, 64, 64)),
        "pool_scale": f(np.asarray(pool_scale).reshape(2, 128).T),
        "fox_q_g": rep2(fox_q_g),
        "fox_k_g": rep2(fox_k_g),
        "mem_norm_g": f(np.asarray(mem_norm_g).reshape(1, D)),
        "w_mem_kv": f(np.asarray(w_mem_kv).reshape(D, 512)),
        "mem_q_g": rep2(mem_q_g),
        "mem_k_g": rep2(mem_k_g),
        "w_out": f(np.asarray(w_out).reshape(D, D)),
    }
    in_maps = []
    for b in range(B):
        m = dict(shared)
        m["x"] = x[b]
        m["mem"] = mem[b]
        in_maps.append(m)
    res = run_bass_kernel_spmd(nc, in_maps, core_ids=list(range(B)))
    return np.stack([np.asarray(r["out"], dtype=np.float32).reshape(SEQ, D) for r in res.results], axis=0)
```

```python
import numpy as np
from contextlib import ExitStack
import concourse.bass as bass
import concourse.mybir as mybir
from concourse.bass_utils import run_bass_kernel_spmd

F32 = mybir.dt.float32
BF16 = mybir.dt.bfloat16
AF = mybir.ActivationFunctionType
ALU = mybir.AluOpType

SEQ = 4096
D = 1024
NT = SEQ // 128
NB = SEQ // 512
EPS = 1e-6
NEG = -30000.0
C_UA, C_GA, C_Q, C_K, C_V, C_F, C_GB, C_QM, C_GM = 0, 256, 512, 1024, 1536, 2048, 2056, 2568, 2824
IN_W = 3080


_LAST = {}


class Sched:
    def __init__(self, nc, stack, needed=None):
        self.needed = needed
        self.needed_out = set()
        self.real = {}
        self.real_map = {}
        self.nc = nc
        self.eng = {"pe": nc.tensor, "act": nc.scalar, "dve": nc.vector, "pool": nc.gpsimd, "sp": nc.sync}
        self.sem = {k: stack.enter_context(nc.semaphore("s_" + k)) for k in self.eng}
        self.cnt = {k: 0 for k in self.eng}
        self.waited = {k: {} for k in self.eng}
        self.last_w = {}
        self.readers = {}
        self.dma_sems = {}
        self.stack = stack
        self.nwait = 0

    def _deps(self, reads, writes):
        deps = []
        for k in reads:
            if k in self.last_w:
                deps.append(self.last_w[k])
        for k in writes:
            if k in self.last_w:
                deps.append(self.last_w[k])
            deps.extend(self.readers.get(k, ()))
        return deps

    def _need(self, e, deps):
        need = {}
        for (sem, val, src) in deps:
            if src == "pe" and e == "pe":
                continue
            key = id(sem)
            if key not in need or need[key][1] < val:
                need[key] = (sem, val, src)
        out = []
        for key, (sem, val, src) in need.items():
            if self.waited[e].get(key, 0) >= val:
                continue
            self.waited[e][key] = val
            if src != "dma":
                en = src[:-2] if src.endswith("_b") else src
                self.needed_out.add((en, val))
                if self.needed is not None:
                    val = self.real_map[en][val]
            out.append((sem, val))
        return out

    def _wait(self, e, deps, ins_fn=None):
        ws = self._need(e, deps)
        if ins_fn is None:
            for (sem, val) in ws:
                self.eng[e].wait_ge(sem, val)
                self.nwait += 1
            return None
        for (sem, val) in ws[:-1]:
            self.eng[e].wait_ge(sem, val)
            self.nwait += 1
        ins = ins_fn()
        if ws:
            ins._wait_ge(ws[-1][0], ws[-1][1])
        return ins

    def _commit(self, ticket, reads, writes):
        for k in reads:
            self.readers.setdefault(k, []).append(ticket)
        for k in writes:
            self.last_w[k] = ticket
            self.readers[k] = []

    PSUM_KEYS = ("pA", "pAm", "pS", "pO", "pN", "pT")

    rec = None

    def record(self, f):
        assert self.rec is None
        self.rec = []
        f()
        r, self.rec = self.rec, None
        return r

    HOP = 0.25
    COST = {"pe": 0.3, "act": 0.65, "dve": 0.7, "pool": 1.2, "sp": 0.05}

    def emit_scheduled(self, lists):
        if not hasattr(self, "sim_eng"):
            self.sim_eng, self.sim_w, self.sim_r = {}, {}, {}
        pos = [0] * len(lists)

        def info(kind, args):
            if kind == "op":
                e, fn, reads, writes, cost = args
            else:
                e, out, in_, reads, writes, key, kw = args
                cost = None
            ps = [k for k in reads if (k[0] if isinstance(k, tuple) else k) in self.PSUM_KEYS]
            rd = [k for k in reads if k not in ps]
            wr = list(writes) + ps
            return e, rd, wr, cost

        while True:
            best = None
            for i, l in enumerate(lists):
                if pos[i] >= len(l):
                    continue
                kind, args = l[pos[i]]
                e, rd, wr, cost = info(kind, args)
                ready = 0.0
                for k in rd:
                    ready = max(ready, self.sim_w.get(k, 0.0))
                for k in wr:
                    ready = max(ready, self.sim_w.get(k, 0.0), self.sim_r.get(k, 0.0))
                start = max(ready + self.HOP, self.sim_eng.get(e, 0.0))
                cand = (start, pos[i] / len(l))
                if best is None or cand < best[0]:
                    best = (cand, i, start, e, rd, wr, cost, kind, args)
            if best is None:
                break
            _, i, start, e, rd, wr, cost, kind, args = best
            pos[i] += 1
            if kind == "op":
                dur = cost if cost is not None else self.COST[e]
                end = start + dur
                self.sim_eng[e] = end
                self.op(*args[:4])
            else:
                self.sim_eng[e] = start + 0.1
                end = start + 3.0
                e_, out, in_, reads, writes, key, kw = args
                self.dma(e_, out, in_, reads, writes, key, **kw)
            for k in rd:
                self.sim_r[k] = max(self.sim_r.get(k, 0.0), end)
            for k in wr:
                self.sim_w[k] = end

    def emit_interleaved(self, lists):
        pos = [0] * len(lists)
        while True:
            best, bf = None, 2.0
            for i, l in enumerate(lists):
                if pos[i] < len(l):
                    fr = pos[i] / len(l)
                    if fr < bf:
                        best, bf = i, fr
            if best is None:
                break
            kind, args = lists[best][pos[best]]
            pos[best] += 1
            if kind == "op":
                self.op(*args[:4])
            else:
                e, out, in_, reads, writes, key, kw = args
                self.dma(e, out, in_, reads, writes, key, **kw)

    def op(self, e, fn, reads=(), writes=(), cost=None):
        if self.rec is not None:
            self.rec.append(("op", (e, fn, list(reads), list(writes), cost)))
            return
        ps = [k for k in reads if (k[0] if isinstance(k, tuple) else k) in self.PSUM_KEYS]
        if ps:
            reads = [k for k in reads if k not in ps]
            writes = list(writes) + ps
        ins = self._wait(e, self._deps(reads, writes), ins_fn=lambda: fn(self.eng[e]))
        self.cnt[e] += 1
        if self.needed is None or (e, self.cnt[e]) in self.needed:
            self.real[e] = self.real.get(e, 0) + 1
            self.real_map.setdefault(e, {})[self.cnt[e]] = self.real[e]
            ins.then_inc(self.sem[e], 1)
        self._commit((self.sem[e], self.cnt[e], e), reads, writes)

    def dma(self, e, out, in_, reads=(), writes=(), key=None, **kw):
        if key is None:
            key = writes[0]
        if self.rec is not None:
            self.rec.append(("dma", (e, out, in_, list(reads), list(writes), key, kw)))
            return
        if key not in self.dma_sems:
            self.dma_sems[key] = [self.stack.enter_context(self.nc.semaphore("d_%d" % len(self.dma_sems))), 0]
        self._wait(e, self._deps(reads, writes))
        ds = self.dma_sems[key]
        ds[1] += 16
        self.eng[e].dma_start(out=out, in_=in_, **kw).then_inc(ds[0], 16)
        self._commit((ds[0], ds[1], "dma"), reads, writes)

    def barrier(self):
        allt = [(self.sem[k], self.cnt[k], k + "_b") for k in self.eng if self.cnt[k] > 0]
        allt += [(ds[0], ds[1], "dma") for ds in self.dma_sems.values() if ds[1] > 0]
        for e in self.eng:
            self._wait(e, allt)
        self.last_w = {}
        self.readers = {}


class Rot:
    def __init__(self, name, tensors):
        self.name, self.t, self.i = name, tensors, 0

    def next(self):
        j = self.i % len(self.t)
        self.i += 1
        return self.t[j], (self.name, j)


def build_nc(debug=False, stop=99, needed=None):
    nc = bass.Bass("TRN2", target_bir_lowering=False)
    dr = lambda n, s, k="ExternalInput": nc.dram_tensor(n, s, F32, kind=k).ap()
    x_d = dr("x", [SEQ, D])
    mem_d = dr("mem", [256, D])
    ng_d = dr("norm_g", [1, D])
    win_d = dr("w_in", [D, IN_W])
    bf_d = dr("b_f", [8, 1])
    wp_d = dr("w_pool", [4, 64, 64])
    ps_d = dr("pool_scale", [128, 2])
    fq_d = dr("fox_q_g", [128, 1])
    fk_d = dr("fox_k_g", [128, 1])
    mg_d = dr("mem_norm_g", [1, D])
    wkv_d = dr("w_mem_kv", [D, 512])
    mq_d = dr("mem_q_g", [128, 1])
    mk_d = dr("mem_k_g", [128, 1])
    wo_d = dr("w_out", [D, D])
    out_d = dr("out", [SEQ, D], "ExternalOutput")
    dbg_d = nc.dram_tensor("dbg", [128, 8 * SEQ], BF16, kind="ExternalOutput").ap() if debug else None

    with ExitStack() as gst:
        S = Sched(nc, gst, needed)
        _LAST["S"] = S
        cnt = [0]

        def sbt(st, shape, dt, name=None):
            cnt[0] += 1
            return st.enter_context(nc.sbuf_tensor(name or ("t%d" % cnt[0]), shape, dt))

        def pst(st, shape, dt, name=None):
            cnt[0] += 1
            return st.enter_context(nc.psum_tensor(name or ("p%d" % cnt[0]), shape, dt))

        ident_bf = sbt(gst, [128, 128], BF16)
        ident_f = sbt(gst, [128, 128], F32)
        bones = sbt(gst, [128, 128], BF16)
        trimask = sbt(gst, [128, 128], BF16)
        neghalf = sbt(gst, [128, 512], F32)
        zt = sbt(gst, [128, 128], F32)
        S.op("pool", lambda e: e.memset(zt[:], 0.0), writes=["zt"])
        S.op("pool", lambda e: e.affine_select(out=ident_f[:], in_=zt[:], pattern=[[-1, 128]], compare_op=ALU.not_equal,
                                               fill=1.0, base=0, channel_multiplier=1), reads=["zt"], writes=["ident_f"])
        S.op("pool", lambda e: e.tensor_copy(out=ident_bf[:], in_=ident_f[:]), reads=["ident_f"], writes=["ident_bf"])
        S.op("pool", lambda e: e.affine_select(out=trimask[:], in_=zt[:], pattern=[[1, 128]], compare_op=ALU.is_ge,
                                               fill=NEG, base=0, channel_multiplier=-1), reads=["zt"], writes=["trimask"])
        S.op("pool", lambda e: e.memset(bones[:], 0.0), writes=["bones"])
        S.op("pool", lambda e: e.memset(bones[0:64, 0:64], 1.0), reads=["bones"], writes=["bones"])
        S.op("pool", lambda e: e.memset(bones[64:128, 64:128], 1.0), reads=["bones"], writes=["bones"])
        S.op("pool", lambda e: e.memset(neghalf[:], -0.5), writes=["neghalf"])
        eps64 = sbt(gst, [128, 1], F32)
        S.op("pool", lambda e: e.memset(eps64[:], 64.0 * EPS), writes=["eps64"])

        gqk = sbt(gst, [128, 1], F32)
        gmqk = sbt(gst, [128, 1], F32)
        pscol = sbt(gst, [128, 2], F32)
        negb = sbt(gst, [8, 1], F32)
        tq = sbt(gst, [128, 4], F32)
        S.dma("sp", tq[:, 0:1], fq_d[:, :], writes=["tq0"])
        S.dma("sp", tq[:, 1:2], fk_d[:, :], writes=["tq1"])
        S.dma("sp", tq[:, 2:3], mq_d[:, :], writes=["tq2"])
        S.dma("sp", tq[:, 3:4], mk_d[:, :], writes=["tq3"])
        S.dma("sp", pscol[:], ps_d[:, :], writes=["pscol"])
        S.dma("sp", negb[:], bf_d[:, :], writes=["negb"])
        S.op("dve", lambda e: e.tensor_tensor(out=gqk[:], in0=tq[:, 0:1], in1=tq[:, 1:2], op=ALU.mult), reads=["tq0", "tq1"], writes=["gqk"])
        S.op("dve", lambda e: e.tensor_tensor(out=gmqk[:], in0=tq[:, 2:3], in1=tq[:, 3:4], op=ALU.mult), reads=["tq2", "tq3"], writes=["gmqk"])
        S.op("dve", lambda e: e.tensor_scalar(out=negb[:], in0=negb[:], scalar1=-1.0, scalar2=None, op0=ALU.mult), reads=["negb"], writes=["negb"])

        hT = sbt(gst, [128, 8, SEQ], BF16, "hT")
        kmT = sbt(gst, [64, 2, 2, 256], BF16, "kmT")
        Vm = sbt(gst, [128, 2, 2, 192], BF16, "Vm")
        R2c = sbt(gst, [128, 512], BF16, "R2c")
        Gk = sbt(gst, [128, 32, 8], F32, "Gk")
        cGb = sbt(gst, [128, 8, 8], F32, "cGb")
        wpbd = sbt(gst, [128, 2, 128], BF16, "wpbd")
        invw = sbt(gst, [128, 2], F32, "invw")
        invc = sbt(gst, [128, 2, 16], F32, "invc")

        for c in range(2):
            for hf in range(2):
                w = (2, 4, 8, 16)[2 * c + hf]
                S.op("pool", lambda e, c=c, hf=hf, w=w: e.memset(invw[hf * 64:(hf + 1) * 64, c:c + 1], 1.0 / w), writes=[("invw", c, hf)])
        iot = sbt(gst, [128, 16], F32)
        S.op("pool", lambda e: e.iota(iot[:], pattern=[[1, 16]], base=1, channel_multiplier=0, allow_small_or_imprecise_dtypes=True), writes=["iot"])
        for c in range(2):
            for hf in range(2):
                w = (2, 4, 8, 16)[2 * c + hf]
                S.op("dve", lambda e, c=c, hf=hf, w=w: e.tensor_scalar(out=invc[hf * 64:(hf + 1) * 64, c, :], in0=iot[hf * 64:(hf + 1) * 64, :],
                                                                     scalar1=float(w), scalar2=None, op0=ALU.min), reads=["iot"], writes=[("invc", c, hf)])
        S.op("dve", lambda e: e.reciprocal(out=invc[:], in_=invc[:]), reads=[("invc", c, hf) for c in range(2) for hf in range(2)], writes=["invc"])
        S.op("pool", lambda e: e.memset(wpbd[:], 0.0), writes=["wpbd"])
        for g in range(4):
            hf, c = g % 2, g // 2
            S.dma("pool", wpbd[hf * 64:(hf + 1) * 64, c, hf * 64:(hf + 1) * 64], wp_d[g, :, :], reads=["wpbd"], writes=[("wpbd", g)])
        wpbd_keys = ["wpbd"] + [("wpbd", g) for g in range(4)]

        def qk_norm(ps_ap, pskey, N, gscal, gkeys, dstA, dstB, dkeyA, dkeyB, sqr, msr, rsr, pN):
            sq, sqk = sqr.next()
            S.op("act", lambda e: e.activation(out=sq[:, 0:N], in_=ps_ap, func=AF.Square), reads=[pskey], writes=[sqk])
            S.op("pe", lambda e: e.matmul(pN[:, 0:N], lhsT=bones[:], rhs=sq[:, 0:N], start=True, stop=True), reads=[sqk, "bones"], writes=["pN"])
            ms, msk = msr.next()
            S.op("act", lambda e: e.activation(out=ms[:, 0:N], in_=pN[:, 0:N], func=AF.Ln, bias=eps64[:, 0:1], scale=1.0), reads=["pN", "eps64"], writes=[msk])
            rs, rsk = rsr.next()
            S.op("act", lambda e: e.activation(out=rs[:, 0:N], in_=ms[:, 0:N], func=AF.Exp, scale=-0.5), reads=[msk], writes=[rsk])
            for hf, dst, dk in ((0, dstA, dkeyA), (1, dstB, dkeyB)):
                sc = gscal if isinstance(gscal, float) else gscal[hf * 64:(hf + 1) * 64, 0:1]
                S.op("dve", lambda e, hf=hf, dst=dst, sc=sc: e.scalar_tensor_tensor(
                    out=dst, in0=ps_ap[hf * 64:(hf + 1) * 64, :], scalar=sc, in1=rs[hf * 64:(hf + 1) * 64, 0:N], op0=ALU.mult, op1=ALU.mult),
                    reads=[pskey, rsk] + list(gkeys), writes=[dk])

        def run_attention(tiles, pSr, PTr, pO):
            L = 2
            nt = len(tiles)
            st = [None] * nt
            cur = []
            rate = 0
            for n in range(nt + L):
                if n < nt:
                    t = tiles[n]
                    if t.get("pre"):
                        t["pre"]()
                    if t.get("drain"):
                        while cur:
                            cur.pop(0)()
                    if "bg" in t:
                        while cur:
                            cur.pop(0)()
                        cur = list(t["bg"]())
                        rate = -(-len(cur) // max(1, t["bg_tiles"] - 3))
                    ps, psk = pSr.next()
                    c0, c1 = t["c0"], 512
                    S.op("pe", lambda e, t=t, ps=ps: e.matmul(ps[:, c0:c1], lhsT=t["kT"], rhs=t["qT"], start=True, stop=not t["mask"]),
                         reads=t["kkeys"] + t["qkeys"], writes=[psk])
                    if t["mask"]:
                        S.op("pe", lambda e, ps=ps: e.matmul(ps[:, c0:c0 + 128], lhsT=ident_bf[:], rhs=trimask[:], start=False, stop=True),
                             reads=["ident_bf", "trimask"], writes=[psk])
                    pt, ptk = PTr.next()
                    if t["bias"] is not None:
                        S.op("act", lambda e, t=t, ps=ps, pt=pt: e.activation(out=pt[:, c0:c1], in_=ps[:, c0:c1], func=AF.Exp, bias=t["bias"], scale=1.0),
                             reads=[psk] + t["bkeys"], writes=[ptk])
                    else:
                        S.op("act", lambda e, ps=ps, pt=pt: e.activation(out=pt[:, c0:c1], in_=ps[:, c0:c1], func=AF.Exp), reads=[psk], writes=[ptk])
                    st[n] = (pt, ptk)
                m = n - L
                if m >= 0:
                    t = tiles[m]
                    pt, ptk = st[m]
                    c0, c1 = t["c0"], 512
                    ob = pO[t["ob"]]
                    S.op("pe", lambda e, t=t, pt=pt, ob=ob: e.matmul(ob[:, c0:c1], lhsT=t["v"], rhs=pt[:, c0:c1], start=t["ostart"], stop=t["olast"]),
                         reads=[ptk] + t["vkeys"], writes=[("pO", t["ob"])])
                    if t["olast"]:
                        t["fin"]()
                for _ in range(rate):
                    if cur:
                        cur.pop(0)()
            while cur:
                cur.pop(0)()

        def finalize(ob_ap, obkey, a, mixed_ap, mkey, rden, tmp, act_recip=False):
            lo, hi = a * 64, a * 64 + 64
            dlo, dhi = (64, 128) if a == 0 else (0, 64)
            if act_recip:
                S.op("act", lambda e: e.activation(out=rden[lo:hi, :], in_=ob_ap[dlo:dhi, :], func=AF.Ln), reads=[obkey], writes=[("rden", a)])
                S.op("act", lambda e: e.activation(out=rden[lo:hi, :], in_=rden[lo:hi, :], func=AF.Exp, scale=-1.0), reads=[("rden", a)], writes=[("rden", a)])
            else:
                S.op("dve", lambda e: e.reciprocal(out=rden[lo:hi, :], in_=ob_ap[dlo:dhi, :]), reads=[obkey], writes=[("rden", a)], cost=3.4)
            S.op("dve", lambda e: e.tensor_tensor(out=tmp[lo:hi, :], in0=ob_ap[lo:hi, :], in1=mixed_ap, op=ALU.mult), reads=[obkey, mkey], writes=[("tmpf", a)])
            S.op("pool", lambda e: e.tensor_tensor(out=mixed_ap, in0=tmp[lo:hi, :], in1=rden[lo:hi, :], op=ALU.mult),
                 reads=[("rden", a), ("tmpf", a)], writes=[mkey])

        mixA = sbt(gst, [128, 4, SEQ], BF16, "mixA")

        def make_norm_transpose(junk, hbr, ssr, pTr):
            evac_i = [0]

            def norm_transpose(src, srckey, gt, gtkey, dst3, dstkey):
                ss, ssk = ssr.next()
                hb, hbk = hbr.next()
                S.op("act", lambda e: e.activation(out=hb[:], in_=src, func=AF.Square, accum_out=ss[:, 0:1]), reads=[srckey], writes=[hbk, (ssk, 0)], cost=1.1)
                S.op("dve", lambda e: e.tensor_scalar(out=ss[:, 1:2], in0=ss[:, 0:1], scalar1=1.0 / D, scalar2=EPS, op0=ALU.mult, op1=ALU.add),
                     reads=[(ssk, 0)], writes=[(ssk, 1)], cost=0.15)
                S.op("pool", lambda e: e.tensor_tensor(out=ss[:, 2:3], in0=ss[:, 1:2], in1=neghalf[:, 0:1], op=ALU.pow), reads=[(ssk, 1), "neghalf"], writes=[(ssk, 2)], cost=0.5)
                S.op("dve", lambda e: e.scalar_tensor_tensor(out=hb[:], in0=src, scalar=ss[:, 2:3], in1=gt[:], op0=ALU.mult, op1=ALU.mult),
                     reads=[srckey, (ssk, 2), gtkey], writes=[hbk], cost=1.3)
                pT, pTk = pTr.next()
                for k in range(8):
                    S.op("pe", lambda e, k=k: e.transpose(out=pT[:, k * 128:(k + 1) * 128], in_=hb[:, k * 128:(k + 1) * 128], identity=ident_bf[:]),
                         reads=[hbk, "ident_bf"], writes=[pTk], cost=0.15)
                eng = "act" if evac_i[0] % 2 == 0 else "dve"
                evac_i[0] += 1
                src3 = pT[:].rearrange("p (k t) -> p k t", t=128)
                if eng == "act":
                    S.op("act", lambda e: e.copy(out=dst3, in_=src3), reads=[pTk], writes=[dstkey], cost=1.0)
                else:
                    S.op("dve", lambda e: e.tensor_copy(out=dst3, in_=src3), reads=[pTk], writes=[dstkey], cost=1.1)
            return norm_transpose

        with ExitStack() as pa_:
            gx = sbt(pa_, [128, D], F32)
            S.dma("sp", gx[:], ng_d[0:1, :].to_broadcast([128, D]), writes=["gx"])
            xr = Rot("xt", [sbt(pa_, [128, D], F32) for _ in range(2)])
            junk = None
            hbr = Rot("hb", [sbt(pa_, [128, D], BF16) for _ in range(2)])
            ssr = Rot("ss", [sbt(pa_, [128, 4], F32) for _ in range(2)])
            pTr = Rot("pT", [pst(pa_, [128, 8 * 128], BF16) for _ in range(1)])
            pAr = Rot("pA", [pst(pa_, [128, 512], F32)])
            pAm = Rot("pAm", [pst(pa_, [128, 512], F32)])
            pN = pst(pa_, [128, 512], F32)
            pSr = Rot("pS", [pst(pa_, [128, 512], F32) for _ in range(2)])
            pO = [pst(pa_, [128, 512], F32) for _ in range(2)]
            sqr = Rot("sq", [sbt(pa_, [128, 512], BF16) for _ in range(2)])
            msr = Rot("ms", [sbt(pa_, [128, 512], F32)])
            rsr = Rot("rs", [sbt(pa_, [128, 512], F32)])
            PTr = Rot("PT", [sbt(pa_, [128, 512], BF16) for _ in range(3)])
            rden = sbt(pa_, [128, 512], F32)
            tmpf = sbt(pa_, [128, 512], F32)
            QAmb = [sbt(pa_, [64, 2, 2, 512], BF16) for _ in range(2)]
            U = sbt(pa_, [128, 528], F32)
            S2 = sbt(pa_, [128, 528], F32)
            S4 = sbt(pa_, [128, 528], F32)
            S8 = sbt(pa_, [128, 528], F32)
            Uh = sbt(pa_, [128, 2, 16], F32)
            t16 = sbt(pa_, [128, 16], F32)
            Dr = Rot("dd", [sbt(pa_, [128, 512], BF16) for _ in range(2)])
            norm_transpose = make_norm_transpose(junk, hbr, ssr, pTr)
            memT = sbt(pa_, [128, 8, 256], BF16, "memT")
            gm = sbt(pa_, [128, D], F32)
            xm = sbt(pa_, [128, 2, D], F32)
            wkv = sbt(pa_, [128, 8, 512], BF16)
            S.dma("sp", gm[:], mg_d[0:1, :].to_broadcast([128, D]), writes=["gm"])
            S.dma("sp", xm[:], mem_d.rearrange("(a p) d -> p a d", p=128), writes=["xm"])
            S.dma("pool", wkv[:], wkv_d.rearrange("(k p) c -> p k c", p=128), writes=["wkv"])
            for a in range(2):
                norm_transpose(xm[:, a, :], "xm", gm, "gm", memT[:, :, a * 128:(a + 1) * 128], ("memT", a))
            memk = [("memT", 0), ("memT", 1)]

            def mem_kv():
                S.op("pool", lambda e: e.memset(Vm[:], 1.0), writes=["Vm1"])
                for c in range(2):
                    pa, pak = pAm.next()
                    for k in range(8):
                        S.op("pe", lambda e, k=k, c=c, pa=pa: e.matmul(pa[:, 0:256], lhsT=wkv[:, k, c * 128:(c + 1) * 128], rhs=memT[:, k, :], start=(k == 0), stop=(k == 7)),
                             reads=["wkv"] + memk, writes=[pak])
                    qk_norm(pa[:, 0:256], pak, 256, 8.0, [], kmT[:, c, 0, :], kmT[:, c, 1, :], ("kmT", c, 0), ("kmT", c, 1), sqr, msr, rsr, pN)
                for J in range(2):
                    pa, pak = pAm.next()
                    for k in range(8):
                        S.op("pe", lambda e, k=k, J=J, pa=pa: e.matmul(pa[:, 0:256], lhsT=memT[:, k, J * 128:(J + 1) * 128], rhs=wkv[:, k, 256:512], start=(k == 0), stop=(k == 7)),
                             reads=["wkv"] + memk, writes=[pak])
                    src = pa[:, 0:256].rearrange("p (c a d) -> p c a d", c=2, a=2)
                    S.op("dve", lambda e, J=J, src=src: e.tensor_copy(out=Vm[:, J, :, 0:64], in_=src[:, :, 0, :]), reads=[pak, "Vm1"], writes=[("Vm", J, 0)])
                    S.op("dve", lambda e, J=J, src=src: e.tensor_copy(out=Vm[:, J, :, 128:192], in_=src[:, :, 1, :]), reads=[pak, "Vm1"], writes=[("Vm", J, 1)])
            wa = []
            for c0 in (C_UA, C_UA + 128, C_GA, C_GA + 128, C_QM, C_QM + 128, C_GM, C_GM + 128):
                w = sbt(pa_, [128, 8, 128], BF16)
                S.dma("pool", w[:], win_d[:, c0:c0 + 128].rearrange("(k p) c -> p k c", p=128), writes=[("wa", c0)])
                wa.append((w, ("wa", c0)))
            w_u, w_g, w_qm, w_gm = wa[0:2], wa[2:4], wa[4:6], wa[6:8]
            S.op("pool", lambda e: e.memset(Uh[:], 0.0), writes=["Uh"])

            def hk(b):
                return [("hT", 4 * b + j) for j in range(4)]

            def proj_a(w, wk, b, pa, pak):
                for k in range(8):
                    S.op("pe", lambda e, k=k: e.matmul(pa[:, :], lhsT=w[:, k, :], rhs=hT[:, k, b * 512:(b + 1) * 512], start=(k == 0), stop=(k == 7)),
                         reads=[wk] + hk(b), writes=[pak])

            def x_block(b):
                for j in range(4):
                    i = 4 * b + j
                    xt, xk = xr.next()
                    S.dma("sp", xt[:], x_d[i * 128:(i + 1) * 128, :], writes=[xk])
                    norm_transpose(xt[:], xk, gx, "gx", hT[:, :, i * 128:(i + 1) * 128], ("hT", i))

            def pool_block(b):
                blk = slice(b * 512, (b + 1) * 512)
                for c in range(2):
                    pa, pak = pAr.next()
                    proj_a(w_u[c][0], w_u[c][1], b, pa, pak)
                    S.op("act", lambda e, pa=pa: e.copy(out=U[:, 16:528], in_=pa[:, :]), reads=[pak], writes=["U"])
                    S.op("pool", lambda e, c=c: e.tensor_copy(out=U[:, 0:16], in_=Uh[:, c, :]), reads=["Uh", "U"], writes=["U"])
                    S.op("pool", lambda e: e.tensor_tensor(out=S2[:, 1:528], in0=U[:, 1:528], in1=U[:, 0:527], op=ALU.add), reads=["U"], writes=["S2"])
                    S.op("pool", lambda e: e.tensor_tensor(out=S4[:, 3:528], in0=S2[:, 3:528], in1=S2[:, 1:526], op=ALU.add), reads=["S2"], writes=["S4"])
                    if c == 1:
                        S.op("pool", lambda e: e.tensor_tensor(out=S8[:, 7:528], in0=S4[:, 7:528], in1=S4[:, 3:524], op=ALU.add), reads=["S4"], writes=["S8"])
                        S.op("pool", lambda e: e.tensor_tensor(out=S2[64:128, 15:528], in0=S8[64:128, 15:528], in1=S8[64:128, 7:520], op=ALU.add),
                             reads=["S8", "S4", "S2"], writes=["S2"])
                    srcs = ((0, S2), (1, S4)) if c == 0 else ((0, S8), (1, S2))
                    dd, ddk = Dr.next()
                    for (hf, Wt) in srcs:
                        sl = slice(hf * 64, (hf + 1) * 64)
                        S.op("dve", lambda e, c=c, sl=sl, Wt=Wt: e.scalar_tensor_tensor(out=dd[sl, :], in0=Wt[sl, 16:528], scalar=invw[sl, c:c + 1],
                                                                                        in1=U[sl, 16:528], op0=ALU.mult, op1=ALU.subtract),
                             reads=["U", "S2", "S4", "S8", ("invw", c, hf)], writes=[(ddk, hf)])
                        if b == 0:
                            S.op("dve", lambda e, c=c, sl=sl, Wt=Wt: e.tensor_tensor(out=t16[sl, :], in0=Wt[sl, 16:32], in1=invc[sl, c, :], op=ALU.mult),
                                 reads=["S2", "S4", "S8", "invc"], writes=[("t16", hf)])
                            S.op("dve", lambda e, sl=sl: e.tensor_tensor(out=dd[sl, 0:16], in0=t16[sl, :], in1=U[sl, 16:32], op=ALU.subtract),
                                 reads=[("t16", hf), "U", (ddk, hf)], writes=[(ddk, hf)])
                    S.op("pool", lambda e, c=c: e.tensor_copy(out=Uh[:, c, :], in_=U[:, 512:528]), reads=["U", "Uh"], writes=["Uh"])
                    pg, pgk = pAr.next()
                    proj_a(w_g[c][0], w_g[c][1], b, pg, pgk)
                    S.op("act", lambda e, c=c, pg=pg: e.activation(out=mixA[:, c, blk], in_=pg[:, :], func=AF.Silu), reads=[pgk], writes=[("mx", c, b)])
                    py, pyk = pAr.next()
                    S.op("pe", lambda e, c=c, py=py: e.matmul(py[:, :], lhsT=wpbd[:, c, :], rhs=dd[:, :], start=True, stop=True),
                         reads=[(ddk, 0), (ddk, 1)] + wpbd_keys, writes=[pyk])
                    S.op("dve", lambda e, c=c, py=py: e.scalar_tensor_tensor(out=mixA[:, c, blk], in0=py[:, :], scalar=pscol[:, c:c + 1], in1=mixA[:, c, blk],
                                                                             op0=ALU.mult, op1=ALU.mult), reads=[pyk, ("mx", c, b), "pscol", "U"], writes=[("mx", c, b)])

            def mem_proj(b):
                blk = slice(b * 512, (b + 1) * 512)
                QAm = QAmb[b % 2]
                for c in range(2):
                    pa, pak = pAm.next()
                    proj_a(w_qm[c][0], w_qm[c][1], b, pa, pak)
                    qk_norm(pa[:, :], pak, 512, gmqk, ["gmqk"], QAm[:, c, 0, :], QAm[:, c, 1, :], ("QAm", b % 2, c, 0), ("QAm", b % 2, c, 1), sqr, msr, rsr, pN)
                for c in range(2):
                    pg, pgk = pAm.next()
                    proj_a(w_gm[c][0], w_gm[c][1], b, pg, pgk)
                    S.op("act", lambda e, c=c, pg=pg: e.activation(out=mixA[:, 2 + c, blk], in_=pg[:, :], func=AF.Silu), reads=[pgk],
                         writes=[("mx", 6 + c, b, 0), ("mx", 6 + c, b, 1)])

            def mem_attn(b):
                blk = slice(b * 512, (b + 1) * 512)
                QAm = QAmb[b % 2]
                tiles = []
                for c in range(2):
                    for a in range(2):
                        oi = ocount_m[0] % 2
                        ocount_m[0] += 1
                        for J in range(2):
                            t = dict(kT=kmT[0:64, c, a, J * 128:(J + 1) * 128], kkeys=[("kmT", c, a)], qT=QAm[:, c, a, :], qkeys=[("QAm", b % 2, c, a)],
                                     c0=0, mask=False, bias=None, bkeys=[], v=Vm[:, J, c, a * 64:a * 64 + 128], vkeys=["Vm1", ("Vm", J, 0), ("Vm", J, 1)],
                                     ob=oi, ostart=(J == 0), olast=(J == 1))
                            if J == 1:
                                t["fin"] = (lambda a=a, c=c, oi=oi: finalize(pO[oi], ("pO", oi), a, mixA[a * 64:(a + 1) * 64, 2 + c, blk],
                                                                             ("mx", 6 + c, b, a), rden, tmpf, act_recip=True))
                            tiles.append(t)
                run_attention(tiles, pSr, PTr, pO)

            ocount_m = [0]
            for step in range(NB + 3):
                lists = []
                if step < NB:
                    lists.append(S.record(lambda: x_block(step)))
                if step == 0:
                    lists.append(S.record(mem_kv))
                if 1 <= step <= NB:
                    lists.append(S.record(lambda: pool_block(step - 1)))
                    lists.append(S.record(lambda: mem_proj(step - 1)))
                if 2 <= step <= NB + 1:
                    lists.append(S.record(lambda: mem_attn(step - 2)))
                S.emit_scheduled(lists)
            S.barrier()

        late = ExitStack()
        mixB = sbt(late, [128, 4, SEQ], BF16, "mixB")
        with ExitStack() as pg_:
            pA = [pst(pg_, [128, 512], F32) for _ in range(2)]
            wf = sbt(pg_, [128, 8, 8], BF16)
            with nc.allow_non_contiguous_dma("tiny f-gate weight slice"):
                S.dma("pool", wf[:], win_d[:, C_F:C_F + 8].rearrange("(k p) c -> p k c", p=128), writes=["wf"])
            onesf = sbt(pg_, [8, 1], F32)
            S.op("pool", lambda e: e.memset(onesf[:], 1.0), writes=["onesf"])
            A = sbt(pg_, [8, SEQ], F32)
            G = sbt(pg_, [8, SEQ], F32)
            Rhi = sbt(pg_, [8, SEQ], BF16)
            Rlo = sbt(pg_, [8, SEQ], BF16)
            Crep = sbt(pg_, [8, NB, 128], F32)
            pAgs = [pst(pg_, [128, 512], F32) for _ in range(3)]
            wgs = []
            for p in range(4):
                w = sbt(pg_, [128, 8, 128], BF16)
                S.dma("pool", w[:], win_d[:, C_GB + 128 * p:C_GB + 128 * (p + 1)].rearrange("(k p) c -> p k c", p=128), writes=[("wg", p)])
                wgs.append(w)

            def gates_list():
                gi_ = 0
                for b in range(NB):
                    for p in range(4):
                        blk = slice(b * 512, (b + 1) * 512)
                        pAg = pAgs[gi_ % 3]
                        pk = ("pAm", gi_ % 3)
                        gi_ += 1
                        for k in range(8):
                            S.op("pe", lambda e, k=k, p=p, b=b, pAg=pAg: e.matmul(pAg[:, :], lhsT=wgs[p][:, k, :], rhs=hT[:, k, b * 512:(b + 1) * 512], start=(k == 0), stop=(k == 7)),
                                 reads=[("wg", p)], writes=[pk], cost=0.25)
                        S.op("act", lambda e, p=p, blk=blk, pAg=pAg: e.activation(out=mixB[:, p, blk], in_=pAg[:, :], func=AF.Silu), reads=[pk],
                             writes=[("mx", 2 + p, b, 0), ("mx", 2 + p, b, 1)])
            def gchain():
                for b in range(NB):
                    pa = pA[b % 2]
                    for k in range(8):
                        S.op("pe", lambda e, k=k, b=b, pa=pa: e.matmul(pa[0:8, :], lhsT=wf[:, k, :], rhs=hT[:, k, b * 512:(b + 1) * 512], start=(k == 0), stop=(k == 7)),
                             reads=["wf"], writes=[("pA", b % 2)])
                    S.op("act", lambda e, b=b, pa=pa: e.activation(out=A[:, b * 512:(b + 1) * 512], in_=pa[0:8, :], func=AF.Exp, bias=negb[:, 0:1], scale=-1.0),
                         reads=[("pA", b % 2), "negb"], writes=[("A", b)])
                S.op("act", lambda e: e.activation(out=A[:], in_=A[:], func=AF.Ln, bias=1.0, scale=1.0), reads=[("A", b) for b in range(NB)], writes=["A"])
                S.op("dve", lambda e: e.tensor_tensor_scan(out=G[:], data0=onesf[:, 0:1].to_broadcast([8, SEQ]), data1=A[:], initial=0.0, op0=ALU.mult, op1=ALU.add),
                     reads=["A", "onesf"], writes=["G"])
                G3 = G[:].rearrange("p (i t) -> p i t", t=512)
                A3 = A[:].rearrange("p (i t) -> p i t", t=512)
                S.op("dve", lambda e: e.tensor_tensor(out=A3, in0=G3[:, :, 0:1].to_broadcast([8, NB, 512]), in1=G3, op=ALU.subtract), reads=["G", "A"], writes=["A"])
                S.op("dve", lambda e: e.tensor_copy(out=Rhi[:], in_=A[:]), reads=["A"], writes=["Rhi"])
                S.op("dve", lambda e: e.tensor_tensor(out=Rlo[:], in0=A[:], in1=Rhi[:], op=ALU.subtract), reads=["A", "Rhi"], writes=["Rlo"])
                for I in range(NB):
                    for j, (Rt, rk) in enumerate(((Rhi, "Rhi"), (Rlo, "Rlo"))):
                        pp = I * 16 + j * 8
                        S.dma("sp", R2c[pp:pp + 8, :], Rt[0:8, I * 512:(I + 1) * 512], reads=[rk], writes=[("R2c", I, j)], key=("R2c", (I * 2 + j) % 4))
            def gchain_b():
                G3 = G[:].rearrange("p (i t) -> p i t", t=512)
                pG = pA[0]
                for J in range(32):
                    S.op("pe", lambda e, J=J: e.transpose(out=pG[:, J * 8:(J + 1) * 8], in_=G[0:8, J * 128:(J + 1) * 128], identity=ident_f[0:8, 0:8]),
                         reads=["G", "ident_f", ("A", 7)], writes=[("pA", 0)])
                S.op("dve", lambda e: e.tensor_copy(out=Gk[:].rearrange("p j h -> p (j h)"), in_=pG[:, 0:256]), reads=[("pA", 0)], writes=["Gk"])
                S.op("dve", lambda e: e.tensor_copy(out=Crep[:], in_=G3[:, :, 0:1].to_broadcast([8, NB, 128])), reads=["G"], writes=["Crep"])
                pC = pA[1]
                for I in range(NB):
                    S.op("pe", lambda e, I=I: e.transpose(out=pC[:, I * 8:(I + 1) * 8], in_=Crep[:, I, :], identity=ident_f[0:8, 0:8]),
                         reads=["Crep", "ident_f", ("A", 7)], writes=[("pA", 1)])
                S.op("dve", lambda e: e.tensor_copy(out=cGb[:].rearrange("p i h -> p (i h)"), in_=pC[:, 0:64]), reads=[("pA", 1)], writes=["cGb"])

            gchain()
            gates_list()
            gchain_b()
            S.barrier()

        with ExitStack() as p2w:
            pAr = Rot("pA", [pst(p2w, [128, 512], F32) for _ in range(2)])
            pN = pst(p2w, [128, 512], F32)
            pSr = Rot("pS", [pst(p2w, [128, 512], F32) for _ in range(3)])
            pO = [pst(p2w, [128, 512], F32) for _ in range(2)]
            sqr = Rot("sq", [sbt(p2w, [128, 512], BF16) for _ in range(2)])
            msr = Rot("ms", [sbt(p2w, [128, 512], F32)])
            rsr = Rot("rs", [sbt(p2w, [128, 512], F32)])
            wr = Rot("w", [sbt(p2w, [128, 8, 128], BF16) for _ in range(6)])
            PTr = Rot("PT", [sbt(p2w, [128, 512], BF16) for _ in range(4)])
            rden = sbt(p2w, [128, 512], F32)
            tmpf = sbt(p2w, [128, 512], F32)
            QA = [sbt(p2w, [66, 2, 512], BF16) for _ in range(2)]
            gcount = [0]
            ocount = [0]

            def load_w(c0):
                w, wk = wr.next()
                S.dma("pool", w[:], win_d[:, c0:c0 + 128].rearrange("(k p) c -> p k c", p=128), writes=[wk])
                return w, wk

            def proj_fm(w, wk, b, pa, pak):
                for k in range(8):
                    S.op("pe", lambda e, k=k: e.matmul(pa[:, :], lhsT=w[:, k, :], rhs=hT[:, k, b * 512:(b + 1) * 512], start=(k == 0), stop=(k == 7)),
                         reads=[wk], writes=[pak])

            KA = sbt(p2w, [66, 2, SEQ], BF16, "KA")
            Vp = sbt(p2w, [128, 32, 192], BF16, "Vp")
            bkb = [sbt(p2w, [128, 2, 32, 8], F32) for _ in range(2)]
            W = {}

            def load_pair_w(p):
                wk_ = load_w(C_K + 128 * p)
                wq_ = load_w(C_Q + 128 * p)
                wv_ = load_w(C_V + 128 * p)
                W[p] = (wq_, wk_, wv_)

            def proj_thunks(w, wk, b, ctx, per=2):
                th = []

                def first():
                    ctx["pa"], ctx["pak"] = pAr.next()
                th.append(first)
                for k0 in range(0, 8, per):
                    def f(k0=k0):
                        pa, pak = ctx["pa"], ctx["pak"]
                        for k in range(k0, k0 + per):
                            S.op("pe", lambda e, k=k: e.matmul(pa[:, :], lhsT=w[:, k, :], rhs=hT[:, k, b * 512:(b + 1) * 512], start=(k == 0), stop=(k == 7)),
                                 reads=[wk], writes=[pak])
                    th.append(f)
                return th

            def norm_thunks(ctx, gscal, gkeys, dstA, dstB, dkeyA, dkeyB):
                N = 512
                th = []

                def f1():
                    ctx["sq"], ctx["sqk"] = sqr.next()
                    S.op("act", lambda e: e.activation(out=ctx["sq"][:, 0:N], in_=ctx["pa"][:, :], func=AF.Square), reads=[ctx["pak"]], writes=[ctx["sqk"]])

                def f2():
                    S.op("pe", lambda e: e.matmul(pN[:, 0:N], lhsT=bones[:], rhs=ctx["sq"][:, 0:N], start=True, stop=True), reads=[ctx["sqk"], "bones"], writes=["pN"])

                def f3():
                    ctx["ms"], ctx["msk"] = msr.next()
                    S.op("act", lambda e: e.activation(out=ctx["ms"][:, 0:N], in_=pN[:, 0:N], func=AF.Ln, bias=eps64[:, 0:1], scale=1.0), reads=["pN", "eps64"], writes=[ctx["msk"]])

                def f4():
                    ctx["rs"], ctx["rsk"] = rsr.next()
                    S.op("act", lambda e: e.activation(out=ctx["rs"][:, 0:N], in_=ctx["ms"][:, 0:N], func=AF.Exp, scale=-0.5), reads=[ctx["msk"]], writes=[ctx["rsk"]])

                def f5(hf, dst, dk):
                    sc = gscal if isinstance(gscal, float) else gscal[hf * 64:(hf + 1) * 64, 0:1]
                    S.op("dve", lambda e: e.scalar_tensor_tensor(out=dst, in0=ctx["pa"][hf * 64:(hf + 1) * 64, :], scalar=sc, in1=ctx["rs"][hf * 64:(hf + 1) * 64, 0:N],
                                                                 op0=ALU.mult, op1=ALU.mult), reads=[ctx["pak"], ctx["rsk"]] + list(gkeys), writes=[dk])
                th += [f1, f2, f3, f4, lambda: f5(0, dstA, dkeyA), lambda: f5(1, dstB, dkeyB)]
                return th

            def block_thunks(p, I, gi):
                th = []
                blk = slice(I * 512, (I + 1) * 512)
                bk = bkb[p % 2]
                if I == 0:
                    def fb():
                        for a in range(2):
                            h = 2 * p + a
                            S.op("dve", lambda e, a=a, h=h: e.tensor_tensor(out=bk[:, a, :, :], in0=Gk[:, :, h].unsqueeze(2).to_broadcast([128, 32, 8]),
                                                                            in1=cGb[:, :, h].unsqueeze(1).to_broadcast([128, 32, 8]), op=ALU.subtract),
                                 reads=[], writes=[("bk", p % 2, a)])
                    th.append(fb)
                if I == 1 and p + 1 < 4:
                    th.append(lambda: load_pair_w(p + 1))
                if p == 1:
                    th.append(lambda: S.dma("sp", out_d[I * 512:(I + 1) * 512, :], x_d[I * 512:(I + 1) * 512, :], writes=[("outinit", I)], key=("oinit", I % 2)))
                cq = {}
                qa = QA[gi % 2]
                th += proj_thunks(W[p][0][0], W[p][0][1], I, cq)
                th += norm_thunks(cq, gqk, ["gqk"], qa[0:64, 0, :], qa[0:64, 1, :], ("QA", gi % 2, 0), ("QA", gi % 2, 1))

                def fr():
                    for a in range(2):
                        h = 2 * p + a
                        for j in range(2):
                            pp = I * 16 + j * 8 + h
                            S.dma("sp", qa[64 + j:65 + j, a, :], R2c[pp:pp + 1, :], reads=[], writes=[("QAr", gi % 2, a, j)], key=("QAr", gi % 2, a, j))
                th.append(fr)
                ck = {}
                th += proj_thunks(W[p][1][0], W[p][1][1], I, ck) if p in W else []
                th += norm_thunks(ck, 8.0, [], KA[0:64, 0, blk], KA[0:64, 1, blk], ("KA", 0, I), ("KA", 1, I))
                cv = {}

                def v0():
                    cv["pa"], cv["pak"] = pAr.next()
                th.append(v0)
                for j in range(4):
                    def fv(j=j):
                        i = 4 * I + j
                        pa, pak = cv["pa"], cv["pak"]
                        wv = W[p][2]
                        for k in range(8):
                            S.op("pe", lambda e, k=k: e.matmul(pa[:, j * 128:(j + 1) * 128], lhsT=hT[:, k, i * 128:(i + 1) * 128], rhs=wv[0][:, k, :],
                                                               start=(k == 0), stop=(k == 7)), reads=[wv[1]], writes=[pak])
                    th.append(fv)

                def fve(q):
                    pa, pak = cv["pa"], cv["pak"]
                    src = pa[:, :].rearrange("p (j a d) -> p j a d", j=4, a=2)
                    S.op("dve", lambda e: e.tensor_copy(out=Vp[:, 4 * I:4 * I + 4, 128 * q:128 * q + 64], in_=src[:, :, q, :]), reads=[pak], writes=[("Vp", I, q)])
                th += [lambda: fve(0), lambda: fve(1)]
                return th

            load_pair_w(0)
            S.op("dve", lambda e: e.memset(KA[64:66, :, :], 1.0), writes=["KAones"])
            S.op("pool", lambda e: e.memset(Vp[:, :, 64:128], 1.0), writes=["Vpones"])
            blocks = [(p, I) for p in range(4) for I in range(NB)]
            gbase = gcount[0]
            tiles = []
            for bi, (p, I) in enumerate(blocks):
                gi = gbase + bi
                qa = QA[gi % 2]
                bk = bkb[p % 2]
                blk = slice(I * 512, (I + 1) * 512)
                for a in range(2):
                    oi = ocount[0] % 2
                    ocount[0] += 1
                    nJ = 4 * I + 4
                    for J in range(nJ):
                        diag = J >= 4 * I
                        c0 = 128 * (J - 4 * I) if diag else 0
                        t = dict(kT=KA[0:66, a, J * 128:(J + 1) * 128], kkeys=[("KA", a, J // 4), "KAones"], qT=qa[0:66, a, c0:512],
                                 qkeys=[("QA", gi % 2, a), ("QAr", gi % 2, a, 0), ("QAr", gi % 2, a, 1)], c0=c0, mask=diag, bias=bk[:, a, J, I:I + 1], bkeys=[("bk", p % 2, a)],
                                 v=Vp[:, J, a * 64:a * 64 + 128], vkeys=[("Vp", J // 4, 0), ("Vp", J // 4, 1), "Vpones"], ob=oi, ostart=(J == 0), olast=(J == nJ - 1))
                        if J == nJ - 1:
                            t["fin"] = (lambda oi=oi, a=a, p=p, I=I, blk=blk: finalize(pO[oi], ("pO", oi), a, mixB[a * 64:(a + 1) * 64, p, blk],
                                                                                       ("mx", 2 + p, I, a), rden, tmpf))
                        if a == 0 and J == 0:
                            t["drain"] = True
                        if bi + 1 < len(blocks):
                            np_, nI = blocks[bi + 1]
                            if np_ == p:
                                if a == 0 and J == 0:
                                    t["bg"] = (lambda np_=np_, nI=nI, gi=gi: block_thunks(np_, nI, gi + 1))
                                    t["bg_tiles"] = 2 * nJ
                            else:
                                if a == 1 and J == 4:
                                    t["bg"] = (lambda np_=np_, nI=nI, gi=gi: block_thunks(np_, nI, gi + 1))
                                    t["bg_tiles"] = nJ - 4
                        tiles.append(t)
            gcount[0] += len(blocks)
            for f in block_thunks(0, 0, gbase):
                f()
            run_attention(tiles, pSr, PTr, pO)
            S.barrier()
        def mxk(k):
            return mixA[:, k, :] if k < 2 else (mixB[:, k - 2, :] if k < 6 else mixA[:, k - 4, :])

        with ExitStack() as p3:
            pAr = Rot("pA", [pst(p3, [128, 512], F32) for _ in range(4)])
            wo = sbt(p3, [128, 8, D], BF16, "wo")
            for k in range(8):
                S.dma("pool", wo[:, k, :], wo_d[k * 128:(k + 1) * 128, :], writes=[("wo", k)])
            o3 = Rot("o3", [sbt(p3, [128, 2, D], F32) for _ in range(3)])
            for i2 in range(NT // 2):
                ot, ok = o3.next()
                first = {}
                if i2 == 0:
                    banks = [pAr.next() for _ in range(4)]
                    for k in range(8):
                        for q, (pa, pak) in enumerate(banks):
                            a_, n_ = q // 2, q % 2
                            S.op("pe", lambda e, k=k, n_=n_, pa=pa, a_=a_: e.matmul(pa[:, :], lhsT=mxk(k)[:, a_ * 128:(a_ + 1) * 128], rhs=wo[:, k, n_ * 512:(n_ + 1) * 512],
                                                                                   start=(k == 0), stop=(k == 7)), reads=[("wo", k)], writes=[pak])
                    for q, b_ in enumerate(banks):
                        first[(q // 2, q % 2)] = b_
                for a in range(2):
                    i = 2 * i2 + a
                    for n in range(2):
                        if (a, n) in first:
                            pa, pak = first[(a, n)]
                        else:
                            pa, pak = pAr.next()
                            for k in range(8):
                                S.op("pe", lambda e, k=k, n=n, pa=pa, i=i: e.matmul(pa[:, :], lhsT=mxk(k)[:, i * 128:(i + 1) * 128], rhs=wo[:, k, n * 512:(n + 1) * 512],
                                                                                   start=(k == 0), stop=(k == 7)), reads=[("wo", k)], writes=[pak])
                        if n == 0:
                            S.op("act", lambda e, n=n, pa=pa, ot=ot, a=a: e.copy(out=ot[:, a, n * 512:(n + 1) * 512], in_=pa[:, :]), reads=[pak], writes=[(ok, a, n)])
                        else:
                            S.op("dve", lambda e, n=n, pa=pa, ot=ot, a=a: e.tensor_copy(out=ot[:, a, n * 512:(n + 1) * 512], in_=pa[:, :]), reads=[pak], writes=[(ok, a, n)])
                    S.dma("pool", out_d[i * 128:(i + 1) * 128, :], ot[:, a, :], reads=[(ok, a, 0), (ok, a, 1)],
                          writes=[("out", i)], key=("odma", i2 % 3, a), accum_op=ALU.add)
            S.barrier()
        late.close()
    return nc


_NC_CACHE = {}


def build_two_pass():
    build_nc(False)
    needed = set(_LAST["S"].needed_out)
    return build_nc(False, needed=needed)


def kernel(x, mem, norm_g, w_in, b_f, w_pool, pool_scale, fox_q_g, fox_k_g, mem_norm_g, w_mem_kv, mem_q_g, mem_k_g, w_out):
    f = lambda a: np.ascontiguousarray(np.asarray(a, dtype=np.float32))
    x = f(x)
    mem = f(mem)
    B = x.shape[0]
    assert B == 8
    if "nc" not in _NC_CACHE:
        _NC_CACHE["nc"] = build_two_pass()
    nc = _NC_CACHE["nc"]
    rep2 = lambda v: f(np.concatenate([np.asarray(v).reshape(64), np.asarray(v).reshape(64)]).reshape(128, 1))
    shared = {
        "norm_g": f(np.asarray(norm_g).reshape(1, D)),
        "w_in": f(np.asarray(w_in).reshape(D, IN_W)),
        "b_f": f(np.asarray(b_f).reshape(8, 1)),
        "w_pool": f(np.asarray(w_pool).reshape(4, 64, 64)),
        "pool_scale": f(np.asarray(pool_scale).reshape(2, 128).T),
        "fox_q_g": rep2(fox_q_g),
        "fox_k_g": rep2(fox_k_g),
        "mem_norm_g": f(np.asarray(mem_norm_g).reshape(1, D)),
        "w_mem_kv": f(np.asarray(w_mem_kv).reshape(D, 512)),
        "mem_q_g": rep2(mem_q_g),
        "mem_k_g": rep2(mem_k_g),
        "w_out": f(np.asarray(w_out).reshape(D, D)),
    }
    in_maps = []
    for b in range(B):
        m = dict(shared)
        m["x"] = x[b]
        m["mem"] = mem[b]
        in_maps.append(m)
    res = run_bass_kernel_spmd(nc, in_maps, core_ids=list(range(B)))
    return np.stack([np.asarray(r["out"], dtype=np.float32).reshape(SEQ, D) for r in res.results], axis=0)
```
